# Optimizing a Trainium2 kernel written in Bass

```python
import math
import jax, jax.numpy as jnp
from jax import lax
import numpy as np

D_MODEL = 2048
BATCH = 1
SEQ = 8192
DEPTH = 4

HEAD_DIM = 64
SGU_WIDTH = D_MODEL // 2
N_SGU_HEADS = SGU_WIDTH // HEAD_DIM
SGU_HEAD_DIM = SGU_WIDTH // N_SGU_HEADS
CHUNK = 128
ATTN_WIDTH = D_MODEL - SGU_WIDTH
N_Q_HEADS = ATTN_WIDTH // HEAD_DIM
N_KV_HEADS = 4
GROUP = N_Q_HEADS // N_KV_HEADS
WINDOW = 128
BLOCK = 128
NUM_BUCKETS = 32
MAX_DISTANCE = 128
IN_WIDTH = 2 * SGU_WIDTH + N_Q_HEADS * HEAD_DIM + 2 * N_KV_HEADS * HEAD_DIM
D_FF = -(-8 * D_MODEL // (3 * 256)) * 256
EPS = 1e-6
NEG_INF = -1e30

kernel_name = "hybrid_sgu_swa_sink_trunk"


def rms_norm(x, g):
    xf = x.astype(jnp.float32)
    y = xf * lax.rsqrt(jnp.mean(xf * xf, axis=-1, keepdims=True) + EPS)
    return (y * g.astype(jnp.float32)).astype(x.dtype)


def t5_causal_bucket(dist):
    n = jnp.maximum(dist, 0)
    max_exact = NUM_BUCKETS // 2
    nf = jnp.maximum(n, 1).astype(jnp.float32)
    large = max_exact + (jnp.log(nf / max_exact) / math.log(MAX_DISTANCE / max_exact)
                         * (NUM_BUCKETS - max_exact)).astype(jnp.int32)
    large = jnp.minimum(large, NUM_BUCKETS - 1)
    return jnp.where(n < max_exact, n, large)


def chunked_sgu(z, v_norm_g, w_s, b_s):
    b, s, _ = z.shape
    u, v = jnp.split(z, 2, axis=-1)
    v = v.reshape(b, s // CHUNK, CHUNK, N_SGU_HEADS, SGU_HEAD_DIM)
    v = rms_norm(v, v_norm_g)
    w = w_s * jnp.tril(jnp.ones((CHUNK, CHUNK), w_s.dtype))
    gate = jnp.einsum('hts,bcshd->bcthd', w, v) + b_s.T[None, None, :, :, None]
    return u * gate.reshape(b, s, SGU_WIDTH)


def swa_sink_attention(q, k, v, q_norm_g, k_norm_g, sinks, rel_bias):
    b, s, _ = q.shape
    nb = s // BLOCK
    q = rms_norm(q.reshape(b, nb, BLOCK, N_KV_HEADS, GROUP, HEAD_DIM), q_norm_g)
    k = rms_norm(k.reshape(b, nb, BLOCK, N_KV_HEADS, HEAD_DIM), k_norm_g)
    v = v.reshape(b, nb, BLOCK, N_KV_HEADS, HEAD_DIM)

    def band(t):
        prev = jnp.concatenate([jnp.zeros_like(t[:, :1]), t[:, :-1]], axis=1)
        return jnp.concatenate([prev, t], axis=2)

    kb, vb = band(k), band(v)
    scale = 1.0 / math.sqrt(HEAD_DIM)
    scores = jnp.einsum('bnqkgd,bnskd->bnkgqs', q, kb).astype(jnp.float32) * scale

    qi = jnp.arange(BLOCK)[:, None]
    kj = jnp.arange(2 * BLOCK)[None, :]
    dist = qi + BLOCK - kj
    bias = rel_bias[t5_causal_bucket(dist)].astype(jnp.float32)
    bias = jnp.transpose(bias, (2, 0, 1)).reshape(N_KV_HEADS, GROUP, BLOCK, 2 * BLOCK)
    in_window = (dist >= 0) & (dist < WINDOW)
    key_pos = jnp.arange(nb)[:, None] * BLOCK - BLOCK + kj
    valid = in_window[None] & (key_pos >= 0)[:, None, :]
    scores = jnp.where(valid[None, :, None, None], scores + bias, NEG_INF)

    sink = jnp.broadcast_to(
        sinks.astype(jnp.float32).reshape(N_KV_HEADS, GROUP)[None, None, :, :, None, None],
        scores.shape[:-1] + (1,))
    probs = jax.nn.softmax(jnp.concatenate([scores, sink], axis=-1), axis=-1)[..., :-1]
    out = jnp.einsum('bnkgqs,bnskd->bnqkgd', probs.astype(vb.dtype), vb)
    return out.reshape(b, s, ATTN_WIDTH)


def setup_inputs(seed: int = 0) -> dict:
    key = jax.random.key(seed)
    ks = jax.random.split(key, 20)
    f32 = jnp.float32
    nrm = lambda k, shape, sc: jax.random.normal(k, shape, f32) * sc
    gain = lambda k, shape: 1.0 + 0.02 * jax.random.normal(k, shape, f32)
    return {
        "x": jax.random.normal(ks[0], (BATCH, SEQ, D_MODEL), f32),
        "rel_bias": nrm(ks[1], (NUM_BUCKETS, N_Q_HEADS), 0.1),
        "norm1_g": gain(ks[2], (DEPTH, D_MODEL)),
        "w_in": nrm(ks[3], (DEPTH, D_MODEL, IN_WIDTH), D_MODEL ** -0.5),
        "sgu_norm_g": gain(ks[4], (DEPTH, N_SGU_HEADS, SGU_HEAD_DIM)),
        "sgu_w": nrm(ks[5], (DEPTH, N_SGU_HEADS, CHUNK, CHUNK), CHUNK ** -0.5),
        "sgu_b": gain(ks[6], (DEPTH, N_SGU_HEADS, CHUNK)),
        "q_norm_g": gain(ks[7], (DEPTH, HEAD_DIM)),
        "k_norm_g": gain(ks[8], (DEPTH, HEAD_DIM)),
        "sinks": nrm(ks[9], (DEPTH, N_Q_HEADS), 0.5),
        "out_norm_a": gain(ks[10], (DEPTH, SGU_WIDTH)),
        "out_norm_b": gain(ks[11], (DEPTH, ATTN_WIDTH)),
        "w_out": nrm(ks[12], (DEPTH, D_MODEL, D_MODEL), D_MODEL ** -0.5),
        "norm2_g": gain(ks[13], (DEPTH, D_MODEL)),
        "w_gate": nrm(ks[14], (DEPTH, D_MODEL, D_FF), D_MODEL ** -0.5),
        "w_up": nrm(ks[15], (DEPTH, D_MODEL, D_FF), D_MODEL ** -0.5),
        "w_down": nrm(ks[16], (DEPTH, D_FF, D_MODEL), D_FF ** -0.5),
    }


def reference(x, rel_bias, norm1_g, w_in, sgu_norm_g, sgu_w, sgu_b, q_norm_g, k_norm_g,
              sinks, out_norm_a, out_norm_b, w_out, norm2_g, w_gate, w_up, w_down):
    q_end = 2 * SGU_WIDTH + N_Q_HEADS * HEAD_DIM
    k_end = q_end + N_KV_HEADS * HEAD_DIM
    for l in range(DEPTH):
        h = rms_norm(x, norm1_g[l])
        z = h @ w_in[l]
        z_sgu = jax.nn.gelu(z[..., :2 * SGU_WIDTH], approximate=False)
        out_a = chunked_sgu(z_sgu, sgu_norm_g[l], sgu_w[l], sgu_b[l])
        out_b = swa_sink_attention(z[..., 2 * SGU_WIDTH:q_end], z[..., q_end:k_end],
                                   z[..., k_end:], q_norm_g[l], k_norm_g[l], sinks[l],
                                   rel_bias)
        mixed = jnp.concatenate([rms_norm(out_a, out_norm_a[l]),
                                 rms_norm(out_b, out_norm_b[l])], axis=-1)
        x = x + mixed @ w_out[l]
        h2 = rms_norm(x, norm2_g[l])
        x = x + (jax.nn.silu(h2 @ w_gate[l]) * (h2 @ w_up[l])) @ w_down[l]
    return x
```

```python
import numpy as np
from contextlib import ExitStack
import concourse.bass as bass
import concourse.mybir as mybir
from concourse.bass_utils import run_bass_kernel_spmd

F32 = mybir.dt.float32
BF16 = mybir.dt.bfloat16
AF = mybir.ActivationFunctionType
ALU = mybir.AluOpType
AX = mybir.AxisListType

D = 2048
SEQ = 8192
NCORE = 8
TOK = SEQ // NCORE
NB = TOK // 128
DC = D // 128
IN_W = 3584
DFF = 5632
NG = DFF // 512
EPS = 1e-6
NEG = -240000.0
SEM_CH = 1024
SAME_ENG_WINDOW = 3
COL_GROUPS = [6, 4, 5, 2, 0, 3, 1]


DEBUG_STOP = None
DEBUG_ITEMS = None
DEBUG_ON = 99
TRACE = None


class _Stop(Exception):
    pass


def _dbg(name):
    if DEBUG_STOP == name:
        raise _Stop()


class _Op:
    __slots__ = ("eng", "fn", "deps", "dma_key", "dma_ord", "mark", "gidx")


class Prog:
    ENGS = ("pe", "act", "dve", "pool", "sp")

    def __init__(self):
        self.ops = []
        self.last_w = {}
        self.readers = {}
        self.dma_count = {}
        self.barrier_deps = set()
        self.last_on_eng = {}
        self.last_dma = {}

    def add(self, eng, fn, reads=(), writes=(), dma_key=None, after=()):
        idx = len(self.ops)
        deps = set(self.barrier_deps)
        for a in after:
            w = self.last_w.get(a)
            if w is not None:
                deps.add(w)
            deps |= self.readers.get(a, set())
        for r in reads:
            w = self.last_w.get(r)
            if w is not None:
                deps.add(w)
        for w_ in writes:
            w = self.last_w.get(w_)
            if w is not None:
                deps.add(w)
            deps |= self.readers.get(w_, set())
        for r in reads:
            self.readers.setdefault(r, set()).add(idx)
        for w_ in writes:
            self.last_w[w_] = idx
            self.readers[w_] = set()
        op = _Op()
        op.eng = eng
        op.fn = fn
        op.deps = deps
        op.dma_key = dma_key
        op.mark = False
        op.gidx = 0
        op.dma_ord = 0
        if dma_key is not None:
            self.dma_count[dma_key] = self.dma_count.get(dma_key, 0) + 1
            op.dma_ord = self.dma_count[dma_key]
            self.last_dma[dma_key] = idx
        else:
            self.last_on_eng[eng] = idx
        self.ops.append(op)
        return idx

    def barrier(self):
        self.barrier_deps = set(self.last_on_eng.values()) | set(self.last_dma.values())

    def emit(self, nc, stack):
        ops = self.ops
        pos = {}
        cnt_e = {e: 0 for e in self.ENGS}
        for i, op in enumerate(ops):
            pos[i] = cnt_e[op.eng]
            cnt_e[op.eng] += 1

        def needs_wait(ci, di):
            op, dop = ops[ci], ops[di]
            if dop.dma_key is not None:
                return True
            if dop.fn is None:
                return False
            if dop.eng != op.eng:
                return True
            return op.eng != "pe" and (pos[ci] - pos[di]) <= SAME_ENG_WINDOW

        self._needs_wait = needs_wait
        for i, op in enumerate(ops):
            for d in op.deps:
                dop = ops[d]
                if dop.dma_key is None and needs_wait(i, d):
                    dop.mark = True
        cnt = {e: 0 for e in self.ENGS}
        for op in ops:
            if op.dma_key is None and op.mark:
                cnt[op.eng] += 1
                op.gidx = cnt[op.eng]
        eng_sems = {}
        for e in self.ENGS:
            n = (cnt[e] + SEM_CH - 1) // SEM_CH
            eng_sems[e] = [stack.enter_context(nc.semaphore(f"s_{e}_{i}")) for i in range(max(n, 1))]
        dma_sems = {k: stack.enter_context(nc.semaphore(f"d_{i}")) for i, k in enumerate(self.dma_count)}
        block = stack.enter_context(nc.Block())

        def make(engname):
            my = [(i, op) for i, op in enumerate(ops) if op.eng == engname]
            needs_wait = self._needs_wait

            def body(e):
                waited = {}
                for i, op in my:
                    emax = {}
                    for d in op.deps:
                        dop = ops[d]
                        if dop.dma_key is not None:
                            key = ("d", dop.dma_key)
                            val = 16 * dop.dma_ord
                            if waited.get(key, 0) < val:
                                emax[key] = max(emax.get(key, 0), val)
                        elif needs_wait(i, d):
                            key = ("e", dop.eng)
                            if waited.get(key, 0) < dop.gidx:
                                emax[key] = max(emax.get(key, 0), dop.gidx)
                    for key, val in emax.items():
                        if key[0] == "d":
                            e.wait_ge(dma_sems[key[1]], val)
                        else:
                            e.wait_ge(eng_sems[key[1]][(val - 1) // SEM_CH], (val - 1) % SEM_CH + 1)
                        waited[key] = val
                        if TRACE is not None:
                            TRACE.append((engname, i, "wait", key, val))
                    if TRACE is not None:
                        TRACE.append((engname, i, "op", op.dma_key, op.dma_ord if op.dma_key is not None else (op.gidx if op.mark else 0)))
                    if op.fn is None:
                        continue
                    ins = op.fn(e)
                    if op.dma_key is not None:
                        ins.then_inc(dma_sems[op.dma_key], 16)
                    elif op.mark:
                        ins.then_inc(eng_sems[engname][(op.gidx - 1) // SEM_CH], 1)
            return body

        block.tensor(make("pe"))
        block.scalar(make("act"))
        block.vector(make("dve"))
        block.gpsimd(make("pool"))
        block.sync(make("sp"))


def build(nl, last):
    nc = bass.Bass("TRN2", target_bir_lowering=False)
    P = Prog()
    st = ExitStack()

    def din(name, shape):
        return nc.dram_tensor(name, shape, F32, kind="ExternalInput").ap()

    x_own = din("x_own", [TOK, D])
    x_halo = din("x_halo", [128, D])
    halo_flag = din("halo_flag", [128, 4])
    ident_d = din("ident", [128, 128])
    tril_d = din("trilT", [128, 128])
    biasT_d = din("biasT", [128, 2, 16, 128])
    maskT_d = din("maskT", [128, 2, 128])
    w_in = din("w_in", [nl, D, IN_W])
    w_out = din("w_out", [nl, D, D])
    w_gate = din("w_gate", [nl, D, DFF])
    w_up = din("w_up", [nl, D, DFF])
    w_down = din("w_down", [nl, DFF, D])
    sgu_wT_d = din("sgu_wT", [nl, 128, 16, 128])
    g1_d = din("g1", [nl, 128, DC])
    g2_d = din("g2", [nl, 128, DC])
    gmix_d = din("gmix", [nl, 128, DC])
    sgu_g_d = din("sgu_g", [nl, 1024])
    sgu_bT_d = din("sgu_bT", [nl, 128, 16])
    qg_d = din("qg", [nl, 64])
    kg_d = din("kg", [nl, 64])
    sinks_d = din("sinks", [nl, 16])
    y = nc.dram_tensor("y", [TOK, D], F32, kind="ExternalOutput").ap()

    def sb(name, shape, dtype):
        return st.enter_context(nc.sbuf_tensor("sb_" + name, shape, dtype))

    xT = sb("xT", [128, DC, TOK], F32)
    R = sb("R", [128, 16384], BF16)
    slabs = sb("slabs", [128, 4, 8192], BF16)
    bias8 = sb("bias8", [128, 2, 16, 128], BF16)
    stage = sb("stage", [128, 2048], F32)
    gateS = sb("gateS", [128, 4, 512], BF16)
    scrA = sb("scrA", [128, 1, 512], F32)
    xsq = sb("xsq", [128, 2, 512], BF16)
    zf = sb("zf", [128, 2, 512], F32)
    zn = sb("zn", [128, 2, 512], BF16)
    sq = sb("sq", [128, 512], F32)
    qT = sb("qT", [64, 8, 128], BF16)
    PT = sb("PT", [128, 2, 2, 512], BF16)
    wT = sb("wT", [128, 16, 128], BF16)
    sgbc = sb("sgbc", [128, 512], F32)
    ident_f = sb("ident_f", [128, 128], F32)
    ident_b = sb("ident_b", [128, 128], BF16)
    ones_b = sb("ones_b", [128, 128], BF16)
    tril_f = sb("tril_f", [128, 128], F32)
    g1 = sb("g1", [128, DC], F32)
    g2 = sb("g2", [128, DC], F32)
    gmix = sb("gmix", [128, DC], F32)
    sgub = sb("sgub", [128, 16], F32)
    qg = sb("qg", [128, 64], F32)
    kg = sb("kg", [128, 64], F32)
    esink = sb("esink", [128, 16], F32)
    stat = sb("stat", [128, 64], F32)
    hflag = sb("hflag", [128, 4], F32)
    epsT = sb("epsT", [128, 1], F32)

    ps = [st.enter_context(nc.psum_tensor(f"ps{i}", [128, 512], F32)) for i in range(8)]

    hT = R[:, 0:DC * 640].rearrange("p (c t) -> p c t", c=DC)
    mixT = R[:, 0:DC * 512].rearrange("p (c t) -> p c t", c=DC)
    h2T = R[:, 0:DC * 1024].rearrange("p (c t) -> p c t", c=DC)
    out_ab = slabs[:, 2, :].rearrange("p (b f) -> p b f", b=4)
    xTh = slabs[:, 2, 0:4096].bitcast(F32).rearrange("p (c t) -> p c t", c=DC)
    kT = slabs[:, 3, 0:9 * 512].rearrange("p (b k t) -> p b k t", b=9, k=4)
    vaug = slabs[:, 3, 9 * 512:9 * 512 + 9 * 4 * 66].rearrange("p (b k t) -> p b k t", b=9, k=4)
    aT = stage[:, :].bitcast(BF16).rearrange("p (j t) -> p j t", j=4)

    def slab_kn(i):
        return slabs[:, i, :].rearrange("p (k n) -> p k n", k=16)

    def slab_jn(i):
        return slabs[:, i, :].rearrange("p (j n) -> p j n", j=4)

    def psb(i):
        return ps[i][:, :].bitcast(BF16)

    add = P.add

    def pk(bank):
        return [("ps", 2, 0), ("ps", 2, 1)] if bank == 2 else [("ps", bank)]
    MIXK = [("mixT", s_, c_) for s_ in range(4) for c_ in range(DC)]

    add("sp", lambda e: e.dma_start(out=ident_f[:, :], in_=ident_d), writes=["ident_f"], dma_key="c0")
    add("sp", lambda e: e.dma_start(out=tril_f[:, :], in_=tril_d), writes=["tril_f"], dma_key="c1")
    mask_f = sq[:, 0:256].rearrange("p (a q) -> p a q", a=2)
    add("sp", lambda e: e.dma_start(out=mask_f, in_=maskT_d), writes=["sq"], dma_key="c2")
    add("sp", lambda e: e.dma_start(out=hflag[:, :], in_=halo_flag), writes=["hflag"], dma_key="c3")
    add("dve", lambda e: e.tensor_copy(out=ident_b[:, :], in_=ident_f[:, :]), reads=["ident_f"], writes=["ident_b"])
    add("dve", lambda e: e.memset(ones_b[:, :], 1.0), writes=["ones_b"])
    add("dve", lambda e: e.memset(epsT[:, :], EPS), writes=["epsT"])
    for pc in range(2):
        add("sp", lambda e, pc=pc: e.dma_start(out=stage[:, :], in_=biasT_d[:, pc].rearrange("p h q -> p (h q)")),
            writes=["stage"], dma_key="c4")
        add("dve", lambda e, pc=pc: e.scalar_tensor_tensor(
            out=bias8[:, pc], in0=stage[:, :].rearrange("p (h q) -> p h q", h=16), scalar=8.0,
            in1=mask_f[:, pc:pc + 1, :].to_broadcast([128, 16, 128]), op0=ALU.mult, op1=ALU.add),
            reads=["stage", "sq"], writes=["bias8"])

    slab_steps = []
    state = {"issued": 0, "cur": -1, "mixer_done": -1}
    step_layer = []
    buf_of = []

    def plan_layer(li):
        for seg in range(2):
            for cg in COL_GROUPS:
                slab_steps.append(("kn", w_in[li][:, cg * 512:(cg + 1) * 512], "m"))
            for ms in range(4):
                slab_steps.append(("kn", w_out[li][:, ms * 512:(ms + 1) * 512], "m"))
        for g in range(NG):
            slab_steps.append(("kn", w_gate[li][:, g * 512:(g + 1) * 512], "f"))
            slab_steps.append(("kn", w_up[li][:, g * 512:(g + 1) * 512], "f"))
            slab_steps.append(("jn", w_down[li][g * 512:(g + 1) * 512, :], "f"))

    for li in range(nl):
        n0 = len(slab_steps)
        plan_layer(li)
        step_layer.extend([li] * (len(slab_steps) - n0))
    mcount = 0
    fcount = 0
    for kind, src, ring in slab_steps:
        if ring == "m":
            buf_of.append(mcount % 2)
            mcount += 1
        else:
            buf_of.append([2, 3, 0, 1][fcount % 4])
            fcount += 1
    step_emitted = [False] * len(slab_steps)

    def pump(look=3):
        while state["issued"] < len(slab_steps) and state["issued"] <= state["cur"] + look:
            s = state["issued"]
            b = buf_of[s]
            prev = [t for t in range(s) if buf_of[t] == b]
            if prev and not step_emitted[prev[-1]]:
                break
            if slab_steps[s][2] == "f" and b in (2, 3) and step_layer[s] > state["mixer_done"]:
                break
            kind, src, ring = slab_steps[s]
            subs = [("slab", b, q_) for q_ in range(4)]
            if kind == "kn":
                add("pool", lambda e, b=b, src=src: e.dma_start(out=slab_kn(b), in_=src.rearrange("(k p) n -> p k n", p=128)),
                    writes=[("slab", b)], dma_key=("slab", b), after=subs)
            else:
                for q_ in range(4):
                    add("pool", lambda e, b=b, src=src, q_=q_: e.dma_start(
                        out=slab_jn(b)[:, :, q_ * 512:(q_ + 1) * 512],
                        in_=src[:, q_ * 512:(q_ + 1) * 512].rearrange("(j p) n -> p j n", p=128)),
                        writes=[("slab", b, q_)], dma_key=("slab", b), after=[("slab", b)])
            state["issued"] += 1

    def begin_step():
        state["cur"] += 1
        s = state["cur"]
        assert state["issued"] > s, "slab not issued"
        return buf_of[s]

    def end_step():
        step_emitted[state["cur"]] = True

    def transpose_in(src_rows, dst, dkey, tokn=128):
        add("sp", lambda e: e.dma_start(out=stage[:, :], in_=src_rows), writes=["stage"], dma_key="xin")
        for q4 in range(4):
            bank = 3 + q4
            for j in range(4):
                c = q4 * 4 + j
                add("pe", lambda e, bank=bank, j=j, c=c: e.transpose(ps[bank][:, j * 128:(j + 1) * 128], stage[:, c * 128:(c + 1) * 128], ident_f[:, :]),
                    reads=["stage", "ident_f"], writes=[("ps", bank)])
            eng = "act" if q4 % 2 == 0 else "dve"
            if eng == "act":
                add("act", lambda e, bank=bank, q4=q4: e.copy(out=dst[:, q4 * 4:(q4 + 1) * 4, :], in_=ps[bank][:, :].rearrange("p (j t) -> p j t", j=4)),
                    reads=[("ps", bank)], writes=[dkey])
            else:
                add("dve", lambda e, bank=bank, q4=q4: e.tensor_copy(out=dst[:, q4 * 4:(q4 + 1) * 4, :], in_=ps[bank][:, :].rearrange("p (j t) -> p j t", j=4)),
                    reads=[("ps", bank)], writes=[dkey])

    ncount = {"n": 0}

    def norm_tile(src, skeys, gain, dst, dkeys, n, after=()):
        r = 0
        bank = 0
        for c in range(DC):
            xs = c % 2
            add("act", lambda e, c=c, xs=xs: e.activation(out=xsq[:, xs, 0:n], in_=src[:, c, :], func=AF.Square),
                reads=list(skeys), writes=[("xsq", xs)])
            add("pe", lambda e, c=c, xs=xs: e.matmul(ps[bank][:, 0:n], ones_b[:, :], xsq[:, xs, 0:n], start=(c == 0), stop=(c == DC - 1)),
                reads=[("xsq", xs), "ones_b"], writes=[("ps", bank)])
        add("act", lambda e: e.activation(out=scrA[:, r, 0:n], in_=ps[bank][:, 0:n], func=AF.Sqrt, bias=epsT[:, 0:1], scale=1.0 / D),
            reads=[("ps", bank), "epsT"], writes=[("scrA", r)])
        add("dve", lambda e: e.reciprocal(out=scrA[:, r, 0:n], in_=scrA[:, r, 0:n]),
            reads=[("scrA", r)], writes=[("scrA", r)])
        for c in range(DC):
            add("dve", lambda e, c=c: e.scalar_tensor_tensor(out=dst[:, c, :], in0=src[:, c, :], scalar=gain[:, c:c + 1], in1=scrA[:, r, 0:n],
                                                             op0=ALU.mult, op1=ALU.mult),
                reads=list(skeys) + [("scrA", r), "gains"], writes=list(dkeys), after=after)

    def headnorm(src, nh, gbc, outs, okeys, skey):
        w = nh * 64
        add("dve", lambda e: e.tensor_tensor(out=sq[:, 0:w], in0=src, in1=src, op=ALU.mult), reads=[skey], writes=["sq"])
        add("dve", lambda e: e.tensor_reduce(out=stat[:, 0:nh], in_=sq[:, 0:w].rearrange("p (h d) -> p h d", h=nh), axis=AX.X, op=ALU.add),
            reads=["sq"], writes=["stat"])
        add("act", lambda e: e.activation(out=stat[:, 0:nh], in_=stat[:, 0:nh], func=AF.Sqrt, bias=epsT[:, 0:1], scale=1.0 / 64),
            reads=["stat", "epsT"], writes=["stat"])
        add("dve", lambda e: e.reciprocal(out=stat[:, 0:nh], in_=stat[:, 0:nh]),
            reads=["stat"], writes=["stat"])
        s3 = src.rearrange("p (h d) -> p h d", h=nh)
        add("dve", lambda e: e.tensor_tensor(out=s3, in0=s3, in1=stat[:, 0:nh].unsqueeze(2).to_broadcast([128, nh, 64]), op=ALU.mult),
            reads=[skey, "stat"], writes=[skey])
        for o in outs:
            add("dve", lambda e, o=o: e.tensor_tensor(out=o, in0=s3, in1=gbc, op=ALU.mult),
                reads=[skey, "lconst"], writes=list(okeys))

    def layer_consts(li):
        add("dve", lambda e: e.memset(vaug[:, :, :, 64:66], 1.0), writes=[("slab", 3)])
        add("dve", lambda e: e.tensor_copy(out=vaug[:, 0, :, 64:65], in_=hflag[:, :].unsqueeze(2)),
            reads=["hflag"], writes=[("slab", 3)])
        add("sp", lambda e: e.dma_start(out=g1[:, :], in_=g1_d[li]), writes=["gains"], dma_key="lc0")
        add("sp", lambda e: e.dma_start(out=g2[:, :], in_=g2_d[li]), writes=["gains"], dma_key="lc1")
        add("sp", lambda e: e.dma_start(out=gmix[:, :], in_=gmix_d[li]), writes=["gains"], dma_key="lc2")
        add("sp", lambda e: e.dma_start(out=sgub[:, :], in_=sgu_bT_d[li]), writes=["lconst"], dma_key="lc3")
        add("sp", lambda e: e.dma_start(out=qg[:, :], in_=qg_d[li:li + 1, :].partition_broadcast(128)), writes=["lconst"], dma_key="lc4")
        add("sp", lambda e: e.dma_start(out=kg[:, :], in_=kg_d[li:li + 1, :].partition_broadcast(128)), writes=["lconst"], dma_key="lc5")
        add("sp", lambda e: e.dma_start(out=esink[:, :], in_=sinks_d[li:li + 1, :].partition_broadcast(128)), writes=["esink"], dma_key="lc6")
        add("act", lambda e: e.activation(out=esink[:, :], in_=esink[:, :], func=AF.Exp), reads=["esink"], writes=["esink"])
        add("sp", lambda e: e.dma_start(out=stage[:, :], in_=sgu_wT_d[li].rearrange("p h t -> p (h t)")), writes=["stage"], dma_key="lc7")
        add("dve", lambda e: e.tensor_tensor(out=wT[:, :, :], in0=stage[:, :].rearrange("p (h t) -> p h t", h=16),
                                             in1=tril_f[:, :].unsqueeze(1).to_broadcast([128, 16, 128]), op=ALU.mult),
            reads=["stage", "tril_f"], writes=["wT"])

    def mixer_segment(li, seg, first_layer):
        b0 = seg * 4
        tok0 = b0 * 128
        if seg == 0:
            if first_layer:
                norm_tile(xTh, [("slab", 2)], g1, hT[:, :, 0:128], [("hT", 0)], 128, after=MIXK)
        norm_tile(xT[:, :, tok0:tok0 + 512], [("xT", seg)], g1, hT[:, :, 128:640], [("hT", 1)], 512, after=MIXK)

        _dbg("norm1_%d" % seg)
        items = []
        for cg in COL_GROUPS:
            slots = list(range(0 if (seg == 0) else 1, 5)) if cg == 6 else list(range(1, 5))
            for sl in slots:
                items.append((cg, sl, sl == slots[0], sl == slots[-1]))

        if DEBUG_ITEMS is not None and seg == 1:
            items = items[:DEBUG_ITEMS]
        cur = {"buf": None}
        zp_i = {"n": 0}

        def stage_A(it):
            cg, sl, first, lastb = it
            if first:
                cur["buf"] = begin_step()
                pump()
            b = cur["buf"]
            attn = cg in (4, 5)
            if attn or zp_i.get("prev_attn", True):
                bank = 0
            else:
                bank = 1 - zp_i["prev_bank"]
            zp_i["prev_attn"] = attn
            zp_i["prev_bank"] = bank
            W = slab_kn(b)
            hk = ("hT", 0 if sl == 0 else 1)
            for k in range(DC):
                add("pe", lambda e, k=k, bank=bank, W=W, sl=sl: e.matmul(ps[bank][:, :], hT[:, k, sl * 128:(sl + 1) * 128], W[:, k, :], start=(k == 0), stop=(k == DC - 1)),
                    reads=[hk, ("slab", b)], writes=[("ps", bank)])
            if lastb:
                end_step()
            return bank

        def stage_B(it, bank, idx):
            cg, sl, first, lastb = it
            blk = b0 + sl - 1
            kidx = blk + 1
            r = idx % 2
            zk = ("zf", r)
            if cg == 6:
                add("act", lambda e: e.copy(out=zf[:, r, 0:256], in_=ps[bank][:, 0:256]), reads=[("ps", bank)], writes=[zk])
                add("act", lambda e: e.copy(out=vaug[:, kidx, :, 0:64], in_=ps[bank][:, 256:512].rearrange("p (k d) -> p k d", k=4)),
                    reads=[("ps", bank)], writes=[("slab", 3)])
                yield
                headnorm(zf[:, r, 0:256], 4, kg[:, :].unsqueeze(1).to_broadcast([128, 4, 64]),
                         [zn[:, r, 0:256].rearrange("p (k d) -> p k d", k=4)], [("zn", r)], zk)
                for kv in range(4):
                    add("pe", lambda e, kv=kv: e.transpose(psb(2)[0:64, r * 512 + kv * 128:r * 512 + (kv + 1) * 128], zn[:, r, kv * 64:(kv + 1) * 64], ident_b[:, :]),
                        reads=[("zn", r), "ident_b"], writes=[("ps", 2, 0), ("ps", 2, 1)])
                add("act", lambda e: e.copy(out=kT[0:64, kidx], in_=psb(2)[0:64, r * 512:(r + 1) * 512].rearrange("p (k t) -> p k t", k=4)),
                    reads=[("ps", 2, 0), ("ps", 2, 1)], writes=[("slab", 3)])
            elif cg in (4, 5):
                qh = cg - 4
                add("act", lambda e: e.copy(out=zf[:, r, :], in_=ps[bank][:, :]), reads=[("ps", bank)], writes=[zk])
                yield
                headnorm(zf[:, r, :], 8, qg[:, :].unsqueeze(1).to_broadcast([128, 8, 64]),
                         [zn[:, r, :].rearrange("p (h d) -> p h d", h=8)], [("zn", r)], zk)
                for hl in range(8):
                    add("pe", lambda e, hl=hl: e.transpose(psb(2)[0:64, hl * 128:(hl + 1) * 128], zn[:, r, hl * 64:(hl + 1) * 64], ident_b[:, :]),
                        reads=[("zn", r), "ident_b"], writes=[("ps", 2, 0), ("ps", 2, 1)])
                add("dve", lambda e: e.tensor_copy(out=qT[0:64, :, :], in_=psb(2)[0:64, :].rearrange("p (j t) -> p j t", j=8)),
                    reads=[("ps", 2, 0), ("ps", 2, 1)], writes=["qT"])
                for kvl in range(2):
                    kv = 2 * qh + kvl
                    for pc in range(2):
                        bk = 3 + 2 * kvl + pc
                        ki = kidx - 1 + pc
                        Sv = ps[bk][:, :].rearrange("p (g t) -> p g t", g=4)
                        add("pe", lambda e, bk=bk, pc=pc, kv=kv: e.matmul(ps[bk][:, :], ident_b[:, :], bias8[:, pc, 4 * kv:4 * kv + 4, :].rearrange("p g q -> p (g q)"), start=True, stop=False),
                            reads=["ident_b", "bias8"], writes=[("ps", bk)])
                        add("pe", lambda e, bk=bk, ki=ki, kv=kv, kvl=kvl: e.matmul(
                            ps[bk][:, :], kT[0:64, ki, kv, :], qT[0:64, 4 * kvl:4 * kvl + 4, :].rearrange("p j t -> p (j t)"),
                            start=False, stop=True),
                            reads=[("slab", 3), "qT"], writes=[("ps", bk)])
                        add("act", lambda e, bk=bk, kvl=kvl, pc=pc: e.activation(out=PT[:, kvl, pc, :], in_=ps[bk][:, :], func=AF.Exp, scale=0.125),
                            reads=[("ps", bk)], writes=[("PT", kvl, pc)])
                for kvl in range(2):
                    kv = 2 * qh + kvl
                    ob = 1 if kvl == 0 else 7
                    Ov = ps[ob][:, 0:264].rearrange("p (g d) -> p g d", g=4)
                    for g in range(4):
                        for pc in range(2):
                            ki = kidx - 1 + pc
                            add("pe", lambda e, Ov=Ov, g=g, pc=pc, ki=ki, kv=kv, kvl=kvl: e.matmul(
                                Ov[:, g, 0:65], PT[:, kvl, pc, g * 128:(g + 1) * 128], vaug[:, ki, kv, 0:65], start=(pc == 0), stop=(pc == 1)),
                                reads=[("PT", kvl, pc), ("slab", 3)], writes=[("ps", ob)])
                    sc = 16 + 4 * kvl
                    add("dve", lambda e, Ov=Ov, sc=sc, kv=kv: e.tensor_tensor(out=stat[:, sc:sc + 4], in0=Ov[:, :, 64:65].rearrange("p g o -> p (g o)"), in1=esink[:, 4 * kv:4 * kv + 4], op=ALU.add),
                        reads=[("ps", ob), "esink"], writes=[("stat2", kvl)])
                    add("dve", lambda e, sc=sc: e.reciprocal(out=stat[:, sc:sc + 4], in_=stat[:, sc:sc + 4]), reads=[("stat2", kvl)], writes=[("stat2", kvl)])
                    add("dve", lambda e, Ov=Ov, sc=sc, kv=kv: e.tensor_tensor(
                        out=out_ab[:, sl - 1, 1024 + kv * 256:1024 + (kv + 1) * 256].rearrange("p (g d) -> p g d", g=4),
                        in0=Ov[:, :, 0:64], in1=stat[:, sc:sc + 4].unsqueeze(2).to_broadcast([128, 4, 64]), op=ALU.mult),
                        reads=[("ps", ob), ("stat2", kvl)], writes=[("slab", 2)])
            elif cg in (2, 3):
                vh = cg - 2
                if first:
                    add("sp", lambda e: e.dma_start(out=sgbc[:, :], in_=sgu_g_d[li:li + 1, vh * 512:(vh + 1) * 512].partition_broadcast(128)),
                        writes=["sgbc"], dma_key="sgbc")
                add("act", lambda e: e.activation(out=zf[:, r, :], in_=ps[bank][:, :], func=AF.Gelu), reads=[("ps", bank)], writes=[zk])
                yield
                w = 512
                add("dve", lambda e: e.tensor_tensor(out=sq[:, 0:w], in0=zf[:, r, :], in1=zf[:, r, :], op=ALU.mult), reads=[zk], writes=["sq"])
                add("dve", lambda e: e.tensor_reduce(out=stat[:, 0:8], in_=sq[:, 0:w].rearrange("p (h d) -> p h d", h=8), axis=AX.X, op=ALU.add),
                    reads=["sq"], writes=["stat"])
                add("act", lambda e: e.activation(out=stat[:, 0:8], in_=stat[:, 0:8], func=AF.Sqrt, bias=epsT[:, 0:1], scale=1.0 / 64),
                    reads=["stat", "epsT"], writes=["stat"])
                add("dve", lambda e: e.reciprocal(out=stat[:, 0:8], in_=stat[:, 0:8]),
                    reads=["stat"], writes=["stat"])
                z3 = zf[:, r, :].rearrange("p (h d) -> p h d", h=8)
                add("dve", lambda e: e.tensor_tensor(out=z3, in0=z3, in1=stat[:, 0:8].unsqueeze(2).to_broadcast([128, 8, 64]), op=ALU.mult),
                    reads=[zk, "stat"], writes=[zk])
                add("dve", lambda e: e.tensor_tensor(out=zn[:, r, :], in0=zf[:, r, :], in1=sgbc[:, :], op=ALU.mult),
                    reads=[zk, "sgbc"], writes=[("zn", r)])
                gb = 3 + (idx % 2)
                for h in range(8):
                    add("pe", lambda e, h=h, gb=gb: e.matmul(ps[gb][:, h * 64:(h + 1) * 64], wT[:, vh * 8 + h, :], zn[:, r, h * 64:(h + 1) * 64], start=True, stop=True),
                        reads=["wT", ("zn", r)], writes=[("ps", gb)])
                add("dve", lambda e, gb=gb: e.tensor_tensor(out=gateS[:, sl - 1, :].rearrange("p (h d) -> p h d", h=8),
                                                             in0=ps[gb][:, :].rearrange("p (h d) -> p h d", h=8),
                                                             in1=sgub[:, vh * 8:(vh + 1) * 8].unsqueeze(2).to_broadcast([128, 8, 64]), op=ALU.add),
                    reads=[("ps", gb), "lconst"], writes=[("gateS", sl - 1)])
            else:
                uh = cg
                add("act", lambda e: e.activation(out=zf[:, r, :], in_=ps[bank][:, :], func=AF.Gelu), reads=[("ps", bank)], writes=[zk])
                yield
                add("dve", lambda e: e.tensor_tensor(out=out_ab[:, sl - 1, uh * 512:(uh + 1) * 512], in0=zf[:, r, :], in1=gateS[:, sl - 1, :], op=ALU.mult),
                    reads=[zk, ("gateS", sl - 1)], writes=[("slab", 2)])

        pump(look=1)
        banks = {}
        banks[0] = stage_A(items[0])
        for i in range(len(items)):
            gen = stage_B(items[i], banks[i], i)
            next(gen)
            if i + 1 < len(items):
                banks[i + 1] = stage_A(items[i + 1])
            for _ in gen:
                pass

        _dbg("inproj%d" % seg)
        for sl in range(4):
            for half in range(2):
                for pc_ in range(2):
                    lo_ = half * 1024 + pc_ * 512
                    add("dve", lambda e, sl=sl, lo_=lo_: e.tensor_tensor(out=sq[:, :], in0=out_ab[:, sl, lo_:lo_ + 512], in1=out_ab[:, sl, lo_:lo_ + 512], op=ALU.mult),
                        reads=[("slab", 2)], writes=["sq"])
                    add("dve", lambda e, half=half, pc_=pc_: e.tensor_reduce(out=stat[:, 36 + 2 * half + pc_:37 + 2 * half + pc_], in_=sq[:, :], axis=AX.X, op=ALU.add),
                        reads=["sq"], writes=[("stat4", half, pc_)])
            if DEBUG_ON < 1:
                continue
            add("dve", lambda e: e.tensor_reduce(out=stat[:, 32:34], in_=stat[:, 36:40].rearrange("p (a b) -> p a b", a=2), axis=AX.X, op=ALU.add),
                reads=[("stat4", 0, 0), ("stat4", 0, 1), ("stat4", 1, 0), ("stat4", 1, 1)], writes=[("stat3", 0), ("stat3", 1)])
            add("act", lambda e: e.activation(out=stat[:, 32:34], in_=stat[:, 32:34], func=AF.Sqrt, bias=epsT[:, 0:1], scale=1.0 / 1024),
                reads=[("stat3", 0), ("stat3", 1), "epsT"], writes=[("stat3", 0), ("stat3", 1)])
            add("dve", lambda e: e.reciprocal(out=stat[:, 32:34], in_=stat[:, 32:34]),
                reads=[("stat3", 0), ("stat3", 1)], writes=[("stat3", 0), ("stat3", 1)])
            if DEBUG_ON < 2:
                continue
            for half in range(2):
                add("dve", lambda e, sl=sl, half=half: e.tensor_tensor(out=out_ab[:, sl, half * 1024:(half + 1) * 1024], in0=out_ab[:, sl, half * 1024:(half + 1) * 1024],
                                                                        in1=stat[:, 32 + half:33 + half].to_broadcast([128, 1024]), op=ALU.mult),
                    reads=[("slab", 2), ("stat3", half)], writes=[("slab", 2)])
            if DEBUG_ON < 3:
                continue
            for hb in range(2):
                bank = 3 + 2 * (sl % 2) + hb
                for j in range(8):
                    c = hb * 8 + j
                    add("pe", lambda e, bank=bank, j=j, c=c, sl=sl: e.transpose(psb(bank)[:, j * 128:(j + 1) * 128], out_ab[:, sl, c * 128:(c + 1) * 128], ident_b[:, :]),
                        reads=[("slab", 2), "ident_b"], writes=[("ps", bank)])
                if DEBUG_ON < 4:
                    continue
                add("dve", lambda e, bank=bank, hb=hb, sl=sl: e.tensor_tensor(
                    out=mixT[:, hb * 8:(hb + 1) * 8, sl * 128:(sl + 1) * 128], in0=psb(bank)[:, :].rearrange("p (j t) -> p j t", j=8),
                    in1=gmix[:, hb * 8:(hb + 1) * 8].unsqueeze(2).to_broadcast([128, 8, 128]), op=ALU.mult),
                    reads=[("ps", bank), "gains"], writes=[("mixT", sl, hb * 8 + j_) for j_ in range(8)], after=[("hT", 0), ("hT", 1)])
        if seg == 1:
            state["mixer_done"] = li

        _dbg("outnorm%d" % seg)
        wb = [0, 1, 7]
        wi = 0
        for ms in range(4):
            b = begin_step()
            pump()
            W = slab_kn(b)
            for mi in range(4):
                m = ms * 4 + mi
                bank = wb[wi % 3]
                wi += 1
                for k in range(DC):
                    add("pe", lambda e, k=k, bank=bank, W=W, mi=mi: e.matmul(ps[bank][:, :], W[:, k, mi * 128:(mi + 1) * 128], mixT[:, k, :], start=(k == 0), stop=(k == DC - 1)),
                        reads=[("mixT", s_, k) for s_ in range(4)] + [("slab", b)], writes=[("ps", bank)])
                add("dve", lambda e, bank=bank, m=m: e.tensor_tensor(out=xT[:, m, tok0:tok0 + 512], in0=ps[bank][:, :], in1=xT[:, m, tok0:tok0 + 512], op=ALU.add),
                    reads=[("ps", bank), ("xT", seg)], writes=[("xT", seg)])
            end_step()

    def ffn(li):
        for t in range(2):
            norm_tile(xT[:, :, t * 512:(t + 1) * 512], [("xT", t)], g2, h2T[:, :, t * 512:(t + 1) * 512], [("h2T", t)], 512, after=MIXK + [("hT", 0), ("hT", 1)])
        gi = 0
        for g in range(NG):
            bg = begin_step()
            pump()
            bu = begin_step()
            pump()
            Wg = slab_kn(bg)
            Wu = slab_kn(bu)
            for t in range(2):
                for j in range(4):
                    gb = 1 + (gi % 2)
                    ub = 3 + (gi % 2)
                    r = gi % 2
                    gi += 1
                    for k in range(DC):
                        add("pe", lambda e, k=k, gb=gb, j=j, t=t, Wg=Wg: e.matmul(ps[gb][:, :], Wg[:, k, j * 128:(j + 1) * 128], h2T[:, k, t * 512:(t + 1) * 512], start=(k == 0), stop=(k == DC - 1)),
                            reads=[("h2T", t), ("slab", bg)], writes=pk(gb))
                    for k in range(DC):
                        add("pe", lambda e, k=k, ub=ub, j=j, t=t, Wu=Wu: e.matmul(ps[ub][:, :], Wu[:, k, j * 128:(j + 1) * 128], h2T[:, k, t * 512:(t + 1) * 512], start=(k == 0), stop=(k == DC - 1)),
                            reads=[("h2T", t), ("slab", bu)], writes=[("ps", ub)])
                    add("act", lambda e, gb=gb, r=r: e.activation(out=zf[:, r, :], in_=ps[gb][:, :], func=AF.Silu), reads=pk(gb), writes=[("zf", r)])
                    add("dve", lambda e, ub=ub, r=r, j=j, t=t: e.tensor_tensor(out=aT[:, j, t * 512:(t + 1) * 512], in0=ps[ub][:, :], in1=zf[:, r, :], op=ALU.mult),
                        reads=[("ps", ub), ("zf", r)], writes=["stage"])
            step_emitted[state["cur"] - 1] = True
            end_step()
            bd = begin_step()
            pump()
            Wd = slab_jn(bd)
            di = 0
            for t in range(2):
                for m in range(DC):
                    db = 5 + (di % 3)
                    di += 1
                    for j in range(4):
                        add("pe", lambda e, db=db, j=j, m=m, t=t, Wd=Wd: e.matmul(ps[db][:, :], Wd[:, j, m * 128:(m + 1) * 128], aT[:, j, t * 512:(t + 1) * 512], start=(j == 0), stop=(j == 3)),
                            reads=["stage", ("slab", bd)] + [("slab", bd, q_) for q_ in range(4)], writes=[("ps", db)])
                    add("dve", lambda e, db=db, m=m, t=t: e.tensor_tensor(out=xT[:, m, t * 512:(t + 1) * 512], in0=ps[db][:, :], in1=xT[:, m, t * 512:(t + 1) * 512], op=ALU.add),
                        reads=[("ps", db), ("xT", t)], writes=[("xT", t)])
            end_step()

    try:
        transpose_in(x_halo, xTh, ("slab", 2))
        for b in range(NB):
            transpose_in(x_own[b * 128:(b + 1) * 128, :], xT[:, :, b * 128:(b + 1) * 128], ("xT", b // 4))
        _dbg("load")
        for li in range(nl):
            layer_consts(li)
            for seg in range(2):
                mixer_segment(li, seg, li == 0)
                _dbg("seg%d" % seg)
            P.barrier()
            _dbg("mixer")
            ffn(li)
            P.barrier()
    except _Stop:
        pass
    oi = 0
    for b in range(NB):
        yb = zf[:, :, :].rearrange("p a n -> p (a n)")
        for hh in range(2):
            for q4 in range(2):
                bank = 3 + (oi % 4)
                oi += 1
                for j in range(4):
                    c = hh * 8 + q4 * 4 + j
                    add("pe", lambda e, bank=bank, j=j, c=c, b=b: e.transpose(ps[bank][:, j * 128:(j + 1) * 128], xT[:, c, b * 128:(b + 1) * 128], ident_f[:, :]),
                        reads=[("xT", b // 4), "ident_f"], writes=[("ps", bank)])
                if q4 == 0:
                    add("act", lambda e, bank=bank, q4=q4: e.copy(out=yb[:, q4 * 512:(q4 + 1) * 512], in_=ps[bank][:, :]), reads=[("ps", bank)], writes=[("zf", 0)])
                else:
                    add("dve", lambda e, bank=bank, q4=q4: e.tensor_copy(out=yb[:, q4 * 512:(q4 + 1) * 512], in_=ps[bank][:, :]), reads=[("ps", bank)], writes=[("zf", 1)])
            add("sp", lambda e, b=b, hh=hh: e.dma_start(out=y[b * 128:(b + 1) * 128, hh * 1024:(hh + 1) * 1024], in_=yb),
                reads=[("zf", 0), ("zf", 1)], writes=["y"], dma_key="yout")
    add("sp", None, reads=["y"], writes=["y_done"])
    P.emit(nc, st)
    st.close()
    return nc


def _bucket_table():
    q = np.arange(128)[:, None]
    k = np.arange(256)[None, :]
    dist = q + 128 - k
    n = np.maximum(dist, 0)
    max_exact = 16
    nf = np.maximum(n, 1).astype(np.float32)
    large = max_exact + (np.log(nf / max_exact) / np.log(128 / max_exact) * (32 - max_exact)).astype(np.int32)
    large = np.minimum(large, 31)
    bucket = np.where(n < max_exact, n, large)
    valid = (dist >= 0) & (dist < 128)
    return bucket, valid


_NC_CACHE = {}


def _get_nc(nl):
    if nl not in _NC_CACHE:
        _NC_CACHE[nl] = build(nl, True)
    return _NC_CACHE[nl]


def kernel(x, rel_bias, norm1_g, w_in, sgu_norm_g, sgu_w, sgu_b, q_norm_g, k_norm_g, sinks,
           out_norm_a, out_norm_b, w_out, norm2_g, w_gate, w_up, w_down):
    f32 = np.float32
    x = np.asarray(x, f32)
    L = w_in.shape[0]
    bucket, valid = _bucket_table()
    bias_g = np.asarray(rel_bias, f32)[bucket]
    biasT = np.ascontiguousarray(bias_g.reshape(128, 2, 128, 16).transpose(2, 1, 3, 0))
    maskT = np.ascontiguousarray(np.where(valid, 0.0, NEG).astype(f32).reshape(128, 2, 128).transpose(2, 1, 0))
    trilT = np.ascontiguousarray(np.triu(np.ones((128, 128), f32)))
    ident = np.eye(128, dtype=f32)

    def chunked(v):
        return np.ascontiguousarray(np.asarray(v, f32).reshape(L, DC, 128).transpose(0, 2, 1))

    g1 = chunked(norm1_g)
    g2 = chunked(norm2_g)
    gmix = chunked(np.concatenate([np.asarray(out_norm_a, f32), np.asarray(out_norm_b, f32)], axis=1))
    sgu_wT = np.ascontiguousarray(np.asarray(sgu_w, f32).transpose(0, 3, 1, 2))
    sgu_bT = np.ascontiguousarray(np.asarray(sgu_b, f32).transpose(0, 2, 1))
    sgu_g = np.ascontiguousarray(np.asarray(sgu_norm_g, f32).reshape(L, 1024))

    xs = x.reshape(SEQ, D)
    nl = 1
    nc = _get_nc(nl)
    for l in range(L):
        in_maps = []
        for c in range(NCORE):
            own = xs[c * TOK:(c + 1) * TOK]
            halo = xs[c * TOK - 128:c * TOK] if c > 0 else np.zeros((128, D), f32)
            flag = np.full((128, 4), 1.0 if c > 0 else 0.0, f32)
            in_maps.append({
                "x_own": np.ascontiguousarray(own), "x_halo": np.ascontiguousarray(halo), "halo_flag": flag,
                "ident": ident, "trilT": trilT, "biasT": biasT, "maskT": maskT,
                "w_in": np.asarray(w_in[l:l + 1], f32), "w_out": np.asarray(w_out[l:l + 1], f32),
                "w_gate": np.asarray(w_gate[l:l + 1], f32), "w_up": np.asarray(w_up[l:l + 1], f32),
                "w_down": np.asarray(w_down[l:l + 1], f32),
                "sgu_wT": sgu_wT[l:l + 1], "g1": g1[l:l + 1], "g2": g2[l:l + 1], "gmix": gmix[l:l + 1],
                "sgu_g": sgu_g[l:l + 1], "sgu_bT": sgu_bT[l:l + 1],
                "qg": np.asarray(q_norm_g[l:l + 1], f32), "kg": np.asarray(k_norm_g[l:l + 1], f32),
                "sinks": np.asarray(sinks[l:l + 1], f32),
            })
        res = run_bass_kernel_spmd(nc, in_maps, core_ids=list(range(NCORE)))
        xs = np.concatenate([np.asarray(r["y"], f32) for r in res.results], axis=0)
    return xs.reshape(1, SEQ, D)
```

```python
import numpy as np
from contextlib import ExitStack
import concourse.bass as bass
import concourse.mybir as mybir
from concourse.bass_utils import run_bass_kernel_spmd

F32 = mybir.dt.float32
BF16 = mybir.dt.bfloat16
AF = mybir.ActivationFunctionType
ALU = mybir.AluOpType
AX = mybir.AxisListType

D = 2048
SEQ = 8192
NCORE = 8
TOK = SEQ // NCORE
NB = TOK // 128
DC = D // 128
IN_W = 3584
DFF = 5632
NG = DFF // 512
EPS = 1e-6
NEG = -240000.0
SEM_CH = 1024
SAME_ENG_WINDOW = 3
COL_GROUPS = [6, 4, 5, 2, 0, 3, 1]


DEBUG_STOP = None
DEBUG_ITEMS = None
DEBUG_ON = 99
TRACE = None


class _Stop(Exception):
    pass


def _dbg(name):
    if DEBUG_STOP == name:
        raise _Stop()


class _Op:
    __slots__ = ("eng", "fn", "deps", "dma_key", "dma_ord", "mark", "gidx")


class Prog:
    ENGS = ("pe", "act", "dve", "pool", "sp")

    def __init__(self):
        self.ops = []
        self.last_w = {}
        self.readers = {}
        self.dma_count = {}
        self.barrier_deps = set()
        self.last_on_eng = {}
        self.last_dma = {}

    def add(self, eng, fn, reads=(), writes=(), dma_key=None, after=()):
        idx = len(self.ops)
        deps = set(self.barrier_deps)
        for a in after:
            w = self.last_w.get(a)
            if w is not None:
                deps.add(w)
            deps |= self.readers.get(a, set())
        for r in reads:
            w = self.last_w.get(r)
            if w is not None:
                deps.add(w)
        for w_ in writes:
            w = self.last_w.get(w_)
            if w is not None:
                deps.add(w)
            deps |= self.readers.get(w_, set())
        for r in reads:
            self.readers.setdefault(r, set()).add(idx)
        for w_ in writes:
            self.last_w[w_] = idx
            self.readers[w_] = set()
        op = _Op()
        op.eng = eng
        op.fn = fn
        op.deps = deps
        op.dma_key = dma_key
        op.mark = False
        op.gidx = 0
        op.dma_ord = 0
        if dma_key is not None:
            self.dma_count[dma_key] = self.dma_count.get(dma_key, 0) + 1
            op.dma_ord = self.dma_count[dma_key]
            self.last_dma[dma_key] = idx
        else:
            self.last_on_eng[eng] = idx
        self.ops.append(op)
        return idx

    def barrier(self):
        self.barrier_deps = set(self.last_on_eng.values()) | set(self.last_dma.values())

    def emit(self, nc, stack):
        ops = self.ops
        pos = {}
        cnt_e = {e: 0 for e in self.ENGS}
        for i, op in enumerate(ops):
            pos[i] = cnt_e[op.eng]
            cnt_e[op.eng] += 1

        def needs_wait(ci, di):
            op, dop = ops[ci], ops[di]
            if dop.dma_key is not None:
                return True
            if dop.fn is None:
                return False
            if dop.eng != op.eng:
                return True
            return op.eng != "pe" and (pos[ci] - pos[di]) <= SAME_ENG_WINDOW

        self._needs_wait = needs_wait
        for i, op in enumerate(ops):
            latest = {}
            for d in op.deps:
                dop = ops[d]
                if dop.dma_key is None and needs_wait(i, d):
                    if dop.eng not in latest or pos[d] > pos[latest[dop.eng]]:
                        latest[dop.eng] = d
            for d in latest.values():
                ops[d].mark = True
        cnt = {e: 0 for e in self.ENGS}
        for op in ops:
            if op.dma_key is None and op.mark:
                cnt[op.eng] += 1
                op.gidx = cnt[op.eng]
        eng_sems = {}
        for e in self.ENGS:
            n = (cnt[e] + SEM_CH - 1) // SEM_CH
            eng_sems[e] = [stack.enter_context(nc.semaphore(f"s_{e}_{i}")) for i in range(max(n, 1))]
        dma_sems = {k: stack.enter_context(nc.semaphore(f"d_{i}")) for i, k in enumerate(self.dma_count)}
        block = stack.enter_context(nc.Block())

        def make(engname):
            my = [(i, op) for i, op in enumerate(ops) if op.eng == engname]
            needs_wait = self._needs_wait

            def body(e):
                waited = {}
                for i, op in my:
                    emax = {}
                    for d in op.deps:
                        dop = ops[d]
                        if dop.dma_key is not None:
                            key = ("d", dop.dma_key)
                            val = 16 * dop.dma_ord
                            if waited.get(key, 0) < val:
                                emax[key] = max(emax.get(key, 0), val)
                        elif needs_wait(i, d) and dop.mark:
                            key = ("e", dop.eng)
                            if waited.get(key, 0) < dop.gidx:
                                emax[key] = max(emax.get(key, 0), dop.gidx)
                    for key, val in emax.items():
                        if key[0] == "d":
                            e.wait_ge(dma_sems[key[1]], val)
                        else:
                            e.wait_ge(eng_sems[key[1]][(val - 1) // SEM_CH], (val - 1) % SEM_CH + 1)
                        waited[key] = val
                        if TRACE is not None:
                            TRACE.append((engname, i, "wait", key, val))
                    if TRACE is not None:
                        TRACE.append((engname, i, "op", op.dma_key, op.dma_ord if op.dma_key is not None else (op.gidx if op.mark else 0)))
                    if op.fn is None:
                        continue
                    ins = op.fn(e)
                    if op.dma_key is not None:
                        ins.then_inc(dma_sems[op.dma_key], 16)
                    elif op.mark:
                        ins.then_inc(eng_sems[engname][(op.gidx - 1) // SEM_CH], 1)
            return body

        block.tensor(make("pe"))
        block.scalar(make("act"))
        block.vector(make("dve"))
        block.gpsimd(make("pool"))
        block.sync(make("sp"))


def build(nl, last):
    nc = bass.Bass("TRN2", target_bir_lowering=False)
    P = Prog()
    st = ExitStack()

    def din(name, shape):
        return nc.dram_tensor(name, shape, F32, kind="ExternalInput").ap()

    x_own = din("x_own", [TOK, D])
    x_halo = din("x_halo", [128, D])
    halo_flag = din("halo_flag", [128, 4])
    ident_d = din("ident", [128, 128])
    tril_d = din("trilT", [128, 128])
    biasT_d = din("biasT", [128, 2, 16, 128])
    maskT_d = din("maskT", [128, 2, 128])
    w_in = din("w_in", [nl, D, IN_W])
    w_out = din("w_out", [nl, D, D])
    w_gate = din("w_gate", [nl, D, DFF])
    w_up = din("w_up", [nl, D, DFF])
    w_down = din("w_down", [nl, DFF, D])
    sgu_wT_d = din("sgu_wT", [nl, 128, 16, 128])
    g1_d = din("g1", [nl, 128, DC])
    g2_d = din("g2", [nl, 128, DC])
    gmix_d = din("gmix", [nl, 128, DC])
    sgu_g_d = din("sgu_g", [nl, 1024])
    sgu_bT_d = din("sgu_bT", [nl, 128, 16])
    qg_d = din("qg", [nl, 64])
    kg_d = din("kg", [nl, 64])
    sinks_d = din("sinks", [nl, 16])
    y = nc.dram_tensor("y", [TOK, D], F32, kind="ExternalOutput").ap()

    def sb(name, shape, dtype):
        return st.enter_context(nc.sbuf_tensor("sb_" + name, shape, dtype))

    xT = sb("xT", [128, DC, TOK], F32)
    R = sb("R", [128, 16384], BF16)
    slabs = sb("slabs", [128, 4, 8192], BF16)
    bias8 = sb("bias8", [128, 2, 16, 128], BF16)
    stage = sb("stage", [128, 2048], F32)
    gateS = sb("gateS", [128, 4, 512], BF16)
    scrA = sb("scrA", [128, 1, 512], F32)
    xsq = sb("xsq", [128, 2, 512], BF16)
    zf = sb("zf", [128, 2, 512], F32)
    zn = sb("zn", [128, 2, 512], BF16)
    sq = sb("sq", [128, 512], F32)
    qT = sb("qT", [64, 8, 128], BF16)
    PT = sb("PT", [128, 2, 2, 512], BF16)
    wT = sb("wT", [128, 16, 128], BF16)
    sgbc = sb("sgbc", [128, 512], F32)
    ident_f = sb("ident_f", [128, 128], F32)
    ident_b = sb("ident_b", [128, 128], BF16)
    ones_b = sb("ones_b", [128, 128], BF16)
    tril_f = sb("tril_f", [128, 128], F32)
    g1 = sb("g1", [128, DC], F32)
    g2 = sb("g2", [128, DC], F32)
    gmix = sb("gmix", [128, DC], F32)
    sgub = sb("sgub", [128, 16], F32)
    qg = sb("qg", [128, 64], F32)
    kg = sb("kg", [128, 64], F32)
    esink = sb("esink", [128, 16], F32)
    stat = sb("stat", [128, 64], F32)
    hflag = sb("hflag", [128, 4], F32)
    epsT = sb("epsT", [128, 1], F32)

    ps = [st.enter_context(nc.psum_tensor(f"ps{i}", [128, 512], F32)) for i in range(8)]

    hT = R[:, 0:DC * 640].rearrange("p (c t) -> p c t", c=DC)
    mixT = R[:, 0:DC * 512].rearrange("p (c t) -> p c t", c=DC)
    h2T = R[:, 0:DC * 1024].rearrange("p (c t) -> p c t", c=DC)
    out_ab = slabs[:, 2, :].rearrange("p (b f) -> p b f", b=4)
    xTh = slabs[:, 2, 0:4096].bitcast(F32).rearrange("p (c t) -> p c t", c=DC)
    kT = slabs[:, 3, 0:9 * 512].rearrange("p (b k t) -> p b k t", b=9, k=4)
    vaug = slabs[:, 3, 9 * 512:9 * 512 + 9 * 4 * 66].rearrange("p (b k t) -> p b k t", b=9, k=4)
    aT = stage[:, :].bitcast(BF16).rearrange("p (j t) -> p j t", j=4)

    def slab_kn(i):
        return slabs[:, i, :].rearrange("p (k n) -> p k n", k=16)

    def slab_jn(i):
        return slabs[:, i, :].rearrange("p (j n) -> p j n", j=4)

    def psb(i):
        return ps[i][:, :].bitcast(BF16)

    add = P.add

    def pk(bank):
        return [("ps", 2, 0), ("ps", 2, 1)] if bank == 2 else [("ps", bank)]
    MIXK = [("mixT", s_, c_) for s_ in range(4) for c_ in range(DC)]

    add("sp", lambda e: e.dma_start(out=ident_f[:, :], in_=ident_d), writes=["ident_f"], dma_key="c0")
    add("sp", lambda e: e.dma_start(out=tril_f[:, :], in_=tril_d), writes=["tril_f"], dma_key="c1")
    mask_f = sq[:, 0:256].rearrange("p (a q) -> p a q", a=2)
    add("sp", lambda e: e.dma_start(out=mask_f, in_=maskT_d), writes=["sq"], dma_key="c2")
    add("sp", lambda e: e.dma_start(out=hflag[:, :], in_=halo_flag), writes=["hflag"], dma_key="c3")
    add("dve", lambda e: e.tensor_copy(out=ident_b[:, :], in_=ident_f[:, :]), reads=["ident_f"], writes=["ident_b"])
    add("dve", lambda e: e.memset(ones_b[:, :], 1.0), writes=["ones_b"])
    add("dve", lambda e: e.memset(epsT[:, :], EPS), writes=["epsT"])
    for pc in range(2):
        add("sp", lambda e, pc=pc: e.dma_start(out=stage[:, :], in_=biasT_d[:, pc].rearrange("p h q -> p (h q)")),
            writes=["stage"], dma_key="c4")
        add("dve", lambda e, pc=pc: e.scalar_tensor_tensor(
            out=bias8[:, pc], in0=stage[:, :].rearrange("p (h q) -> p h q", h=16), scalar=8.0,
            in1=mask_f[:, pc:pc + 1, :].to_broadcast([128, 16, 128]), op0=ALU.mult, op1=ALU.add),
            reads=["stage", "sq"], writes=["bias8"])

    slab_steps = []
    state = {"issued": 0, "cur": -1, "mixer_done": -1}
    step_layer = []
    buf_of = []

    def plan_layer(li):
        for seg in range(2):
            for cg in COL_GROUPS:
                slab_steps.append(("kn", w_in[li][:, cg * 512:(cg + 1) * 512], "m"))
            for ms in range(4):
                slab_steps.append(("kn", w_out[li][:, ms * 512:(ms + 1) * 512], "m"))
        for g in range(NG):
            slab_steps.append(("kn", w_gate[li][:, g * 512:(g + 1) * 512], "f"))
            slab_steps.append(("kn", w_up[li][:, g * 512:(g + 1) * 512], "f"))
            slab_steps.append(("jn", w_down[li][g * 512:(g + 1) * 512, :], "f"))

    for li in range(nl):
        n0 = len(slab_steps)
        plan_layer(li)
        step_layer.extend([li] * (len(slab_steps) - n0))
    mcount = 0
    fcount = 0
    for kind, src, ring in slab_steps:
        if ring == "m":
            buf_of.append(mcount % 2)
            mcount += 1
        else:
            buf_of.append([2, 3, 0, 1][fcount % 4])
            fcount += 1
    step_emitted = [False] * len(slab_steps)

    def pump(look=3):
        while state["issued"] < len(slab_steps) and state["issued"] <= state["cur"] + look:
            s = state["issued"]
            b = buf_of[s]
            prev = [t for t in range(s) if buf_of[t] == b]
            if prev and not step_emitted[prev[-1]]:
                break
            if slab_steps[s][2] == "f" and b in (2, 3) and step_layer[s] > state["mixer_done"]:
                break
            kind, src, ring = slab_steps[s]
            subs = [("slab", b, q_) for q_ in range(4)]
            if kind == "kn":
                add("pool", lambda e, b=b, src=src: e.dma_start(out=slab_kn(b), in_=src.rearrange("(k p) n -> p k n", p=128)),
                    writes=[("slab", b)], dma_key=("slab", b), after=subs)
            else:
                for q_ in range(4):
                    add("pool", lambda e, b=b, src=src, q_=q_: e.dma_start(
                        out=slab_jn(b)[:, :, q_ * 512:(q_ + 1) * 512],
                        in_=src[:, q_ * 512:(q_ + 1) * 512].rearrange("(j p) n -> p j n", p=128)),
                        writes=[("slab", b, q_)], dma_key=("slab", b), after=[("slab", b)])
            state["issued"] += 1

    def begin_step():
        state["cur"] += 1
        s = state["cur"]
        assert state["issued"] > s, "slab not issued"
        return buf_of[s]

    def end_step():
        step_emitted[state["cur"]] = True

    def transpose_in(src_rows, dst, dkey, tokn=128):
        add("sp", lambda e: e.dma_start(out=stage[:, :], in_=src_rows), writes=["stage"], dma_key="xin")
        for q4 in range(4):
            bank = 3 + q4
            for j in range(4):
                c = q4 * 4 + j
                add("pe", lambda e, bank=bank, j=j, c=c: e.transpose(ps[bank][:, j * 128:(j + 1) * 128], stage[:, c * 128:(c + 1) * 128], ident_f[:, :]),
                    reads=["stage", "ident_f"], writes=[("ps", bank)])
            eng = "act" if q4 % 2 == 0 else "dve"
            if eng == "act":
                add("act", lambda e, bank=bank, q4=q4: e.copy(out=dst[:, q4 * 4:(q4 + 1) * 4, :], in_=ps[bank][:, :].rearrange("p (j t) -> p j t", j=4)),
                    reads=[("ps", bank)], writes=[dkey])
            else:
                add("dve", lambda e, bank=bank, q4=q4: e.tensor_copy(out=dst[:, q4 * 4:(q4 + 1) * 4, :], in_=ps[bank][:, :].rearrange("p (j t) -> p j t", j=4)),
                    reads=[("ps", bank)], writes=[dkey])

    ncount = {"n": 0}

    def norm_tile(src, skeys, gain, dst, dkeys, n, after=()):
        r = 0
        bank = 0
        for c in range(DC):
            xs = c % 2
            add("act", lambda e, c=c, xs=xs: e.activation(out=xsq[:, xs, 0:n], in_=src[:, c, :], func=AF.Square),
                reads=list(skeys), writes=[("xsq", xs)])
            add("pe", lambda e, c=c, xs=xs: e.matmul(ps[bank][:, 0:n], ones_b[:, :], xsq[:, xs, 0:n], start=(c == 0), stop=(c == DC - 1)),
                reads=[("xsq", xs), "ones_b"], writes=[("ps", bank)])
        add("act", lambda e: e.activation(out=scrA[:, r, 0:n], in_=ps[bank][:, 0:n], func=AF.Sqrt, bias=epsT[:, 0:1], scale=1.0 / D),
            reads=[("ps", bank), "epsT"], writes=[("scrA", r)])
        add("dve", lambda e: e.reciprocal(out=scrA[:, r, 0:n], in_=scrA[:, r, 0:n]),
            reads=[("scrA", r)], writes=[("scrA", r)])
        for c in range(DC):
            add("dve", lambda e, c=c: e.scalar_tensor_tensor(out=dst[:, c, :], in0=src[:, c, :], scalar=gain[:, c:c + 1], in1=scrA[:, r, 0:n],
                                                             op0=ALU.mult, op1=ALU.mult),
                reads=list(skeys) + [("scrA", r), "gains"], writes=list(dkeys), after=after)

    def headnorm(src, nh, gbc, outs, okeys, skey):
        w = nh * 64
        add("dve", lambda e: e.tensor_tensor(out=sq[:, 0:w], in0=src, in1=src, op=ALU.mult), reads=[skey], writes=["sq"])
        add("dve", lambda e: e.tensor_reduce(out=stat[:, 0:nh], in_=sq[:, 0:w].rearrange("p (h d) -> p h d", h=nh), axis=AX.X, op=ALU.add),
            reads=["sq"], writes=["stat"])
        add("act", lambda e: e.activation(out=stat[:, 0:nh], in_=stat[:, 0:nh], func=AF.Sqrt, bias=epsT[:, 0:1], scale=1.0 / 64),
            reads=["stat", "epsT"], writes=["stat"])
        add("dve", lambda e: e.reciprocal(out=stat[:, 0:nh], in_=stat[:, 0:nh]),
            reads=["stat"], writes=["stat"])
        s3 = src.rearrange("p (h d) -> p h d", h=nh)
        add("dve", lambda e: e.tensor_tensor(out=s3, in0=s3, in1=stat[:, 0:nh].unsqueeze(2).to_broadcast([128, nh, 64]), op=ALU.mult),
            reads=[skey, "stat"], writes=[skey])
        for o in outs:
            add("dve", lambda e, o=o: e.tensor_tensor(out=o, in0=s3, in1=gbc, op=ALU.mult),
                reads=[skey, "lconst"], writes=list(okeys))

    def layer_consts(li):
        add("dve", lambda e: e.memset(vaug[:, :, :, 64:66], 1.0), writes=[("slab", 3)])
        add("dve", lambda e: e.tensor_copy(out=vaug[:, 0, :, 64:65], in_=hflag[:, :].unsqueeze(2)),
            reads=["hflag"], writes=[("slab", 3)])
        add("sp", lambda e: e.dma_start(out=g1[:, :], in_=g1_d[li]), writes=["gains"], dma_key="lc0")
        add("sp", lambda e: e.dma_start(out=g2[:, :], in_=g2_d[li]), writes=["gains"], dma_key="lc1")
        add("sp", lambda e: e.dma_start(out=gmix[:, :], in_=gmix_d[li]), writes=["gains"], dma_key="lc2")
        add("sp", lambda e: e.dma_start(out=sgub[:, :], in_=sgu_bT_d[li]), writes=["lconst"], dma_key="lc3")
        add("sp", lambda e: e.dma_start(out=qg[:, :], in_=qg_d[li:li + 1, :].partition_broadcast(128)), writes=["lconst"], dma_key="lc4")
        add("sp", lambda e: e.dma_start(out=kg[:, :], in_=kg_d[li:li + 1, :].partition_broadcast(128)), writes=["lconst"], dma_key="lc5")
        add("sp", lambda e: e.dma_start(out=esink[:, :], in_=sinks_d[li:li + 1, :].partition_broadcast(128)), writes=["esink"], dma_key="lc6")
        add("act", lambda e: e.activation(out=esink[:, :], in_=esink[:, :], func=AF.Exp), reads=["esink"], writes=["esink"])
        add("sp", lambda e: e.dma_start(out=stage[:, :], in_=sgu_wT_d[li].rearrange("p h t -> p (h t)")), writes=["stage"], dma_key="lc7")
        add("dve", lambda e: e.tensor_tensor(out=wT[:, :, :], in0=stage[:, :].rearrange("p (h t) -> p h t", h=16),
                                             in1=tril_f[:, :].unsqueeze(1).to_broadcast([128, 16, 128]), op=ALU.mult),
            reads=["stage", "tril_f"], writes=["wT"])

    def mixer_segment(li, seg, first_layer):
        b0 = seg * 4
        tok0 = b0 * 128
        if seg == 0:
            if first_layer:
                norm_tile(xTh, [("slab", 2)], g1, hT[:, :, 0:128], [("hT", 0)], 128, after=MIXK)
        norm_tile(xT[:, :, tok0:tok0 + 512], [("xT", seg)], g1, hT[:, :, 128:640], [("hT", 1)], 512, after=MIXK)

        _dbg("norm1_%d" % seg)
        items = []
        for cg in COL_GROUPS:
            slots = list(range(0 if (seg == 0) else 1, 5)) if cg == 6 else list(range(1, 5))
            for sl in slots:
                items.append((cg, sl, sl == slots[0], sl == slots[-1]))

        if DEBUG_ITEMS is not None and seg == 1:
            items = items[:DEBUG_ITEMS]
        cur = {"buf": None}
        zp_i = {"n": 0}

        def stage_A(it):
            cg, sl, first, lastb = it
            if first:
                cur["buf"] = begin_step()
                pump()
            b = cur["buf"]
            attn = cg in (4, 5)
            if attn or zp_i.get("prev_attn", True):
                bank = 0
            else:
                bank = 1 - zp_i["prev_bank"]
            zp_i["prev_attn"] = attn
            zp_i["prev_bank"] = bank
            W = slab_kn(b)
            hk = ("hT", 0 if sl == 0 else 1)
            for k in range(DC):
                add("pe", lambda e, k=k, bank=bank, W=W, sl=sl: e.matmul(ps[bank][:, :], hT[:, k, sl * 128:(sl + 1) * 128], W[:, k, :], start=(k == 0), stop=(k == DC - 1)),
                    reads=[hk, ("slab", b)], writes=[("ps", bank)])
            if lastb:
                end_step()
            return bank

        def stage_B(it, bank, idx):
            cg, sl, first, lastb = it
            blk = b0 + sl - 1
            kidx = blk + 1
            r = idx % 2
            zk = ("zf", r)
            if cg == 6:
                add("act", lambda e: e.copy(out=zf[:, r, 0:256], in_=ps[bank][:, 0:256]), reads=[("ps", bank)], writes=[zk])
                add("act", lambda e: e.copy(out=vaug[:, kidx, :, 0:64], in_=ps[bank][:, 256:512].rearrange("p (k d) -> p k d", k=4)),
                    reads=[("ps", bank)], writes=[("slab", 3)])
                yield
                headnorm(zf[:, r, 0:256], 4, kg[:, :].unsqueeze(1).to_broadcast([128, 4, 64]),
                         [zn[:, r, 0:256].rearrange("p (k d) -> p k d", k=4)], [("zn", r)], zk)
                for kv in range(4):
                    add("pe", lambda e, kv=kv: e.transpose(psb(2)[0:64, r * 512 + kv * 128:r * 512 + (kv + 1) * 128], zn[:, r, kv * 64:(kv + 1) * 64], ident_b[:, :]),
                        reads=[("zn", r), "ident_b"], writes=[("ps", 2, 0), ("ps", 2, 1)])
                add("act", lambda e: e.copy(out=kT[0:64, kidx], in_=psb(2)[0:64, r * 512:(r + 1) * 512].rearrange("p (k t) -> p k t", k=4)),
                    reads=[("ps", 2, 0), ("ps", 2, 1)], writes=[("slab", 3)])
            elif cg in (4, 5):
                qh = cg - 4
                add("act", lambda e: e.copy(out=zf[:, r, :], in_=ps[bank][:, :]), reads=[("ps", bank)], writes=[zk])
                yield
                headnorm(zf[:, r, :], 8, qg[:, :].unsqueeze(1).to_broadcast([128, 8, 64]),
                         [zn[:, r, :].rearrange("p (h d) -> p h d", h=8)], [("zn", r)], zk)
                for hl in range(8):
                    add("pe", lambda e, hl=hl: e.transpose(psb(2)[0:64, hl * 128:(hl + 1) * 128], zn[:, r, hl * 64:(hl + 1) * 64], ident_b[:, :]),
                        reads=[("zn", r), "ident_b"], writes=[("ps", 2, 0), ("ps", 2, 1)])
                add("dve", lambda e: e.tensor_copy(out=qT[0:64, :, :], in_=psb(2)[0:64, :].rearrange("p (j t) -> p j t", j=8)),
                    reads=[("ps", 2, 0), ("ps", 2, 1)], writes=["qT"])
                for kvl in range(2):
                    kv = 2 * qh + kvl
                    for pc in range(2):
                        bk = 3 + 2 * kvl + pc
                        ki = kidx - 1 + pc
                        Sv = ps[bk][:, :].rearrange("p (g t) -> p g t", g=4)
                        add("pe", lambda e, bk=bk, pc=pc, kv=kv: e.matmul(ps[bk][:, :], ident_b[:, :], bias8[:, pc, 4 * kv:4 * kv + 4, :].rearrange("p g q -> p (g q)"), start=True, stop=False),
                            reads=["ident_b", "bias8"], writes=[("ps", bk)])
                        add("pe", lambda e, bk=bk, ki=ki, kv=kv, kvl=kvl: e.matmul(
                            ps[bk][:, :], kT[0:64, ki, kv, :], qT[0:64, 4 * kvl:4 * kvl + 4, :].rearrange("p j t -> p (j t)"),
                            start=False, stop=True),
                            reads=[("slab", 3), "qT"], writes=[("ps", bk)])
                        add("act", lambda e, bk=bk, kvl=kvl, pc=pc: e.activation(out=PT[:, kvl, pc, :], in_=ps[bk][:, :], func=AF.Exp, scale=0.125),
                            reads=[("ps", bk)], writes=[("PT", kvl, pc)])
                for kvl in range(2):
                    kv = 2 * qh + kvl
                    ob = 1 if kvl == 0 else 7
                    Ov = ps[ob][:, 0:264].rearrange("p (g d) -> p g d", g=4)
                    for g in range(4):
                        for pc in range(2):
                            ki = kidx - 1 + pc
                            add("pe", lambda e, Ov=Ov, g=g, pc=pc, ki=ki, kv=kv, kvl=kvl: e.matmul(
                                Ov[:, g, 0:65], PT[:, kvl, pc, g * 128:(g + 1) * 128], vaug[:, ki, kv, 0:65], start=(pc == 0), stop=(pc == 1)),
                                reads=[("PT", kvl, pc), ("slab", 3)], writes=[("ps", ob)])
                    sc = 16 + 4 * kvl
                    add("dve", lambda e, Ov=Ov, sc=sc, kv=kv: e.tensor_tensor(out=stat[:, sc:sc + 4], in0=Ov[:, :, 64:65].rearrange("p g o -> p (g o)"), in1=esink[:, 4 * kv:4 * kv + 4], op=ALU.add),
                        reads=[("ps", ob), "esink"], writes=[("stat2", kvl)])
                    add("dve", lambda e, sc=sc: e.reciprocal(out=stat[:, sc:sc + 4], in_=stat[:, sc:sc + 4]), reads=[("stat2", kvl)], writes=[("stat2", kvl)])
                    add("dve", lambda e, Ov=Ov, sc=sc, kv=kv: e.tensor_tensor(
                        out=out_ab[:, sl - 1, 1024 + kv * 256:1024 + (kv + 1) * 256].rearrange("p (g d) -> p g d", g=4),
                        in0=Ov[:, :, 0:64], in1=stat[:, sc:sc + 4].unsqueeze(2).to_broadcast([128, 4, 64]), op=ALU.mult),
                        reads=[("ps", ob), ("stat2", kvl)], writes=[("slab", 2)])
            elif cg in (2, 3):
                vh = cg - 2
                if first:
                    add("sp", lambda e: e.dma_start(out=sgbc[:, :], in_=sgu_g_d[li:li + 1, vh * 512:(vh + 1) * 512].partition_broadcast(128)),
                        writes=["sgbc"], dma_key="sgbc")
                add("act", lambda e: e.activation(out=zf[:, r, :], in_=ps[bank][:, :], func=AF.Gelu), reads=[("ps", bank)], writes=[zk])
                yield
                w = 512
                add("dve", lambda e: e.tensor_tensor(out=sq[:, 0:w], in0=zf[:, r, :], in1=zf[:, r, :], op=ALU.mult), reads=[zk], writes=["sq"])
                add("dve", lambda e: e.tensor_reduce(out=stat[:, 0:8], in_=sq[:, 0:w].rearrange("p (h d) -> p h d", h=8), axis=AX.X, op=ALU.add),
                    reads=["sq"], writes=["stat"])
                add("act", lambda e: e.activation(out=stat[:, 0:8], in_=stat[:, 0:8], func=AF.Sqrt, bias=epsT[:, 0:1], scale=1.0 / 64),
                    reads=["stat", "epsT"], writes=["stat"])
                add("dve", lambda e: e.reciprocal(out=stat[:, 0:8], in_=stat[:, 0:8]),
                    reads=["stat"], writes=["stat"])
                z3 = zf[:, r, :].rearrange("p (h d) -> p h d", h=8)
                add("dve", lambda e: e.tensor_tensor(out=z3, in0=z3, in1=stat[:, 0:8].unsqueeze(2).to_broadcast([128, 8, 64]), op=ALU.mult),
                    reads=[zk, "stat"], writes=[zk])
                add("dve", lambda e: e.tensor_tensor(out=zn[:, r, :], in0=zf[:, r, :], in1=sgbc[:, :], op=ALU.mult),
                    reads=[zk, "sgbc"], writes=[("zn", r)])
                gb = 3 + (idx % 2)
                for h in range(8):
                    add("pe", lambda e, h=h, gb=gb: e.matmul(ps[gb][:, h * 64:(h + 1) * 64], wT[:, vh * 8 + h, :], zn[:, r, h * 64:(h + 1) * 64], start=True, stop=True),
                        reads=["wT", ("zn", r)], writes=[("ps", gb)])
                add("dve", lambda e, gb=gb: e.tensor_tensor(out=gateS[:, sl - 1, :].rearrange("p (h d) -> p h d", h=8),
                                                             in0=ps[gb][:, :].rearrange("p (h d) -> p h d", h=8),
                                                             in1=sgub[:, vh * 8:(vh + 1) * 8].unsqueeze(2).to_broadcast([128, 8, 64]), op=ALU.add),
                    reads=[("ps", gb), "lconst"], writes=[("gateS", sl - 1)])
            else:
                uh = cg
                add("act", lambda e: e.activation(out=zf[:, r, :], in_=ps[bank][:, :], func=AF.Gelu), reads=[("ps", bank)], writes=[zk])
                yield
                add("dve", lambda e: e.tensor_tensor(out=out_ab[:, sl - 1, uh * 512:(uh + 1) * 512], in0=zf[:, r, :], in1=gateS[:, sl - 1, :], op=ALU.mult),
                    reads=[zk, ("gateS", sl - 1)], writes=[("slab", 2)])

        pump(look=1)
        banks = {}
        banks[0] = stage_A(items[0])
        for i in range(len(items)):
            gen = stage_B(items[i], banks[i], i)
            next(gen)
            if i + 1 < len(items):
                banks[i + 1] = stage_A(items[i + 1])
            for _ in gen:
                pass

        _dbg("inproj%d" % seg)
        for sl in range(4):
            for half in range(2):
                for pc_ in range(2):
                    lo_ = half * 1024 + pc_ * 512
                    add("dve", lambda e, sl=sl, lo_=lo_: e.tensor_tensor(out=sq[:, :], in0=out_ab[:, sl, lo_:lo_ + 512], in1=out_ab[:, sl, lo_:lo_ + 512], op=ALU.mult),
                        reads=[("slab", 2)], writes=["sq"])
                    add("dve", lambda e, half=half, pc_=pc_: e.tensor_reduce(out=stat[:, 36 + 2 * half + pc_:37 + 2 * half + pc_], in_=sq[:, :], axis=AX.X, op=ALU.add),
                        reads=["sq"], writes=[("stat4", half, pc_)])
            if DEBUG_ON < 1:
                continue
            add("dve", lambda e: e.tensor_reduce(out=stat[:, 32:34], in_=stat[:, 36:40].rearrange("p (a b) -> p a b", a=2), axis=AX.X, op=ALU.add),
                reads=[("stat4", 0, 0), ("stat4", 0, 1), ("stat4", 1, 0), ("stat4", 1, 1)], writes=[("stat3", 0), ("stat3", 1)])
            add("act", lambda e: e.activation(out=stat[:, 32:34], in_=stat[:, 32:34], func=AF.Sqrt, bias=epsT[:, 0:1], scale=1.0 / 1024),
                reads=[("stat3", 0), ("stat3", 1), "epsT"], writes=[("stat3", 0), ("stat3", 1)])
            add("dve", lambda e: e.reciprocal(out=stat[:, 32:34], in_=stat[:, 32:34]),
                reads=[("stat3", 0), ("stat3", 1)], writes=[("stat3", 0), ("stat3", 1)])
            if DEBUG_ON < 2:
                continue
            for half in range(2):
                add("dve", lambda e, sl=sl, half=half: e.tensor_tensor(out=out_ab[:, sl, half * 1024:(half + 1) * 1024], in0=out_ab[:, sl, half * 1024:(half + 1) * 1024],
                                                                        in1=stat[:, 32 + half:33 + half].to_broadcast([128, 1024]), op=ALU.mult),
                    reads=[("slab", 2), ("stat3", half)], writes=[("slab", 2)])
            if DEBUG_ON < 3:
                continue
            for hb in range(2):
                bank = 3 + 2 * (sl % 2) + hb
                for j in range(8):
                    c = hb * 8 + j
                    add("pe", lambda e, bank=bank, j=j, c=c, sl=sl: e.transpose(psb(bank)[:, j * 128:(j + 1) * 128], out_ab[:, sl, c * 128:(c + 1) * 128], ident_b[:, :]),
                        reads=[("slab", 2), "ident_b"], writes=[("ps", bank)])
                if DEBUG_ON < 4:
                    continue
                add("dve", lambda e, bank=bank, hb=hb, sl=sl: e.tensor_tensor(
                    out=mixT[:, hb * 8:(hb + 1) * 8, sl * 128:(sl + 1) * 128], in0=psb(bank)[:, :].rearrange("p (j t) -> p j t", j=8),
                    in1=gmix[:, hb * 8:(hb + 1) * 8].unsqueeze(2).to_broadcast([128, 8, 128]), op=ALU.mult),
                    reads=[("ps", bank), "gains"], writes=[("mixT", sl, hb * 8 + j_) for j_ in range(8)], after=[("hT", 0), ("hT", 1)])
        if seg == 1:
            state["mixer_done"] = li

        _dbg("outnorm%d" % seg)
        wb = [0, 1, 7]
        wi = 0
        for ms in range(4):
            b = begin_step()
            pump()
            W = slab_kn(b)
            for mi in range(4):
                m = ms * 4 + mi
                bank = wb[wi % 3]
                wi += 1
                for k in range(DC):
                    add("pe", lambda e, k=k, bank=bank, W=W, mi=mi: e.matmul(ps[bank][:, :], W[:, k, mi * 128:(mi + 1) * 128], mixT[:, k, :], start=(k == 0), stop=(k == DC - 1)),
                        reads=[("mixT", s_, k) for s_ in range(4)] + [("slab", b)], writes=[("ps", bank)])
                add("dve", lambda e, bank=bank, m=m: e.tensor_tensor(out=xT[:, m, tok0:tok0 + 512], in0=ps[bank][:, :], in1=xT[:, m, tok0:tok0 + 512], op=ALU.add),
                    reads=[("ps", bank), ("xT", seg)], writes=[("xT", seg)])
            end_step()

    def ffn(li):
        for t in range(2):
            norm_tile(xT[:, :, t * 512:(t + 1) * 512], [("xT", t)], g2, h2T[:, :, t * 512:(t + 1) * 512], [("h2T", t)], 512, after=MIXK + [("hT", 0), ("hT", 1)])
        gi = 0
        for g in range(NG):
            bg = begin_step()
            pump()
            bu = begin_step()
            pump()
            Wg = slab_kn(bg)
            Wu = slab_kn(bu)
            for t in range(2):
                for j in range(4):
                    gb = 1 + (gi % 2)
                    ub = 3 + (gi % 2)
                    r = gi % 2
                    gi += 1
                    for k in range(DC):
                        add("pe", lambda e, k=k, gb=gb, j=j, t=t, Wg=Wg: e.matmul(ps[gb][:, :], Wg[:, k, j * 128:(j + 1) * 128], h2T[:, k, t * 512:(t + 1) * 512], start=(k == 0), stop=(k == DC - 1)),
                            reads=[("h2T", t), ("slab", bg)], writes=pk(gb))
                    for k in range(DC):
                        add("pe", lambda e, k=k, ub=ub, j=j, t=t, Wu=Wu: e.matmul(ps[ub][:, :], Wu[:, k, j * 128:(j + 1) * 128], h2T[:, k, t * 512:(t + 1) * 512], start=(k == 0), stop=(k == DC - 1)),
                            reads=[("h2T", t), ("slab", bu)], writes=[("ps", ub)])
                    add("act", lambda e, gb=gb, r=r: e.activation(out=zf[:, r, :], in_=ps[gb][:, :], func=AF.Silu), reads=pk(gb), writes=[("zf", r)])
                    add("dve", lambda e, ub=ub, r=r, j=j, t=t: e.tensor_tensor(out=aT[:, j, t * 512:(t + 1) * 512], in0=ps[ub][:, :], in1=zf[:, r, :], op=ALU.mult),
                        reads=[("ps", ub), ("zf", r)], writes=["stage"])
            step_emitted[state["cur"] - 1] = True
            end_step()
            bd = begin_step()
            pump()
            Wd = slab_jn(bd)
            di = 0
            for t in range(2):
                for m in range(DC):
                    db = 5 + (di % 3)
                    di += 1
                    for j in range(4):
                        add("pe", lambda e, db=db, j=j, m=m, t=t, Wd=Wd: e.matmul(ps[db][:, :], Wd[:, j, m * 128:(m + 1) * 128], aT[:, j, t * 512:(t + 1) * 512], start=(j == 0), stop=(j == 3)),
                            reads=["stage", ("slab", bd)] + [("slab", bd, q_) for q_ in range(4)], writes=[("ps", db)])
                    add("dve", lambda e, db=db, m=m, t=t: e.tensor_tensor(out=xT[:, m, t * 512:(t + 1) * 512], in0=ps[db][:, :], in1=xT[:, m, t * 512:(t + 1) * 512], op=ALU.add),
                        reads=[("ps", db), ("xT", t)], writes=[("xT", t)])
            end_step()

    try:
        transpose_in(x_halo, xTh, ("slab", 2))
        for b in range(NB):
            transpose_in(x_own[b * 128:(b + 1) * 128, :], xT[:, :, b * 128:(b + 1) * 128], ("xT", b // 4))
        _dbg("load")
        for li in range(nl):
            layer_consts(li)
            for seg in range(2):
                mixer_segment(li, seg, li == 0)
                _dbg("seg%d" % seg)
            P.barrier()
            _dbg("mixer")
            ffn(li)
            P.barrier()
    except _Stop:
        pass
    oi = 0
    for b in range(NB):
        yb = zf[:, :, :].rearrange("p a n -> p (a n)")
        for hh in range(2):
            for q4 in range(2):
                bank = 3 + (oi % 4)
                oi += 1
                for j in range(4):
                    c = hh * 8 + q4 * 4 + j
                    add("pe", lambda e, bank=bank, j=j, c=c, b=b: e.transpose(ps[bank][:, j * 128:(j + 1) * 128], xT[:, c, b * 128:(b + 1) * 128], ident_f[:, :]),
                        reads=[("xT", b // 4), "ident_f"], writes=[("ps", bank)])
                if q4 == 0:
                    add("act", lambda e, bank=bank, q4=q4: e.copy(out=yb[:, q4 * 512:(q4 + 1) * 512], in_=ps[bank][:, :]), reads=[("ps", bank)], writes=[("zf", 0)])
                else:
                    add("dve", lambda e, bank=bank, q4=q4: e.tensor_copy(out=yb[:, q4 * 512:(q4 + 1) * 512], in_=ps[bank][:, :]), reads=[("ps", bank)], writes=[("zf", 1)])
            add("sp", lambda e, b=b, hh=hh: e.dma_start(out=y[b * 128:(b + 1) * 128, hh * 1024:(hh + 1) * 1024], in_=yb),
                reads=[("zf", 0), ("zf", 1)], writes=["y"], dma_key="yout")
    add("sp", None, reads=["y"], writes=["y_done"])
    P.emit(nc, st)
    st.close()
    return nc


def _bucket_table():
    q = np.arange(128)[:, None]
    k = np.arange(256)[None, :]
    dist = q + 128 - k
    n = np.maximum(dist, 0)
    max_exact = 16
    nf = np.maximum(n, 1).astype(np.float32)
    large = max_exact + (np.log(nf / max_exact) / np.log(128 / max_exact) * (32 - max_exact)).astype(np.int32)
    large = np.minimum(large, 31)
    bucket = np.where(n < max_exact, n, large)
    valid = (dist >= 0) & (dist < 128)
    return bucket, valid


_NC_CACHE = {}


def _get_nc(nl):
    if nl not in _NC_CACHE:
        _NC_CACHE[nl] = build(nl, True)
    return _NC_CACHE[nl]


def kernel(x, rel_bias, norm1_g, w_in, sgu_norm_g, sgu_w, sgu_b, q_norm_g, k_norm_g, sinks,
           out_norm_a, out_norm_b, w_out, norm2_g, w_gate, w_up, w_down):
    f32 = np.float32
    x = np.asarray(x, f32)
    L = w_in.shape[0]
    bucket, valid = _bucket_table()
    bias_g = np.asarray(rel_bias, f32)[bucket]
    biasT = np.ascontiguousarray(bias_g.reshape(128, 2, 128, 16).transpose(2, 1, 3, 0))
    maskT = np.ascontiguousarray(np.where(valid, 0.0, NEG).astype(f32).reshape(128, 2, 128).transpose(2, 1, 0))
    trilT = np.ascontiguousarray(np.triu(np.ones((128, 128), f32)))
    ident = np.eye(128, dtype=f32)

    def chunked(v):
        return np.ascontiguousarray(np.asarray(v, f32).reshape(L, DC, 128).transpose(0, 2, 1))

    g1 = chunked(norm1_g)
    g2 = chunked(norm2_g)
    gmix = chunked(np.concatenate([np.asarray(out_norm_a, f32), np.asarray(out_norm_b, f32)], axis=1))
    sgu_wT = np.ascontiguousarray(np.asarray(sgu_w, f32).transpose(0, 3, 1, 2))
    sgu_bT = np.ascontiguousarray(np.asarray(sgu_b, f32).transpose(0, 2, 1))
    sgu_g = np.ascontiguousarray(np.asarray(sgu_norm_g, f32).reshape(L, 1024))

    xs = x.reshape(SEQ, D)
    nl = 1
    nc = _get_nc(nl)
    for l in range(L):
        in_maps = []
        for c in range(NCORE):
            own = xs[c * TOK:(c + 1) * TOK]
            halo = xs[c * TOK - 128:c * TOK] if c > 0 else np.zeros((128, D), f32)
            flag = np.full((128, 4), 1.0 if c > 0 else 0.0, f32)
            in_maps.append({
                "x_own": np.ascontiguousarray(own), "x_halo": np.ascontiguousarray(halo), "halo_flag": flag,
                "ident": ident, "trilT": trilT, "biasT": biasT, "maskT": maskT,
                "w_in": np.asarray(w_in[l:l + 1], f32), "w_out": np.asarray(w_out[l:l + 1], f32),
                "w_gate": np.asarray(w_gate[l:l + 1], f32), "w_up": np.asarray(w_up[l:l + 1], f32),
                "w_down": np.asarray(w_down[l:l + 1], f32),
                "sgu_wT": sgu_wT[l:l + 1], "g1": g1[l:l + 1], "g2": g2[l:l + 1], "gmix": gmix[l:l + 1],
                "sgu_g": sgu_g[l:l + 1], "sgu_bT": sgu_bT[l:l + 1],
                "qg": np.asarray(q_norm_g[l:l + 1], f32), "kg": np.asarray(k_norm_g[l:l + 1], f32),
                "sinks": np.asarray(sinks[l:l + 1], f32),
            })
        res = run_bass_kernel_spmd(nc, in_maps, core_ids=list(range(NCORE)))
        xs = np.concatenate([np.asarray(r["y"], f32) for r in res.results], axis=0)
    return xs.reshape(1, SEQ, D)
```

```python
import numpy as np
from contextlib import ExitStack
import concourse.bass as bass
import concourse.mybir as mybir
from concourse.bass_utils import run_bass_kernel_spmd

F32 = mybir.dt.float32
BF16 = mybir.dt.bfloat16
AF = mybir.ActivationFunctionType
ALU = mybir.AluOpType
AX = mybir.AxisListType

D = 2048
SEQ = 8192
NCORE = 8
TOK = SEQ // NCORE
NB = TOK // 128
DC = D // 128
IN_W = 3584
DFF = 5632
NG = DFF // 512
EPS = 1e-6
NEG = -240000.0
SEM_CH = 1024
SAME_ENG_WINDOW = 3
COL_GROUPS = [6, 4, 5, 2, 0, 3, 1]


DEBUG_STOP = None
DEBUG_ITEMS = None
DEBUG_ON = 99
TRACE = None


class _Stop(Exception):
    pass


def _dbg(name):
    if DEBUG_STOP == name:
        raise _Stop()


class _Op:
    __slots__ = ("eng", "fn", "deps", "dma_key", "dma_ord", "mark", "gidx")


class Prog:
    ENGS = ("pe", "act", "dve", "pool", "sp")

    def __init__(self):
        self.ops = []
        self.last_w = {}
        self.readers = {}
        self.dma_count = {}
        self.barrier_deps = set()
        self.last_on_eng = {}
        self.last_dma = {}

    def add(self, eng, fn, reads=(), writes=(), dma_key=None, after=()):
        idx = len(self.ops)
        deps = set(self.barrier_deps)
        for a in after:
            w = self.last_w.get(a)
            if w is not None:
                deps.add(w)
            deps |= self.readers.get(a, set())
        for r in reads:
            w = self.last_w.get(r)
            if w is not None:
                deps.add(w)
        for w_ in writes:
            w = self.last_w.get(w_)
            if w is not None:
                deps.add(w)
            deps |= self.readers.get(w_, set())
        for r in reads:
            self.readers.setdefault(r, set()).add(idx)
        for w_ in writes:
            self.last_w[w_] = idx
            self.readers[w_] = set()
        op = _Op()
        op.eng = eng
        op.fn = fn
        op.deps = deps
        op.dma_key = dma_key
        op.mark = False
        op.gidx = 0
        op.dma_ord = 0
        if dma_key is not None:
            self.dma_count[dma_key] = self.dma_count.get(dma_key, 0) + 1
            op.dma_ord = self.dma_count[dma_key]
            self.last_dma[dma_key] = idx
        else:
            self.last_on_eng[eng] = idx
        self.ops.append(op)
        return idx

    def barrier(self):
        self.barrier_deps = set(self.last_on_eng.values()) | set(self.last_dma.values())

    def emit(self, nc, stack):
        ops = self.ops
        pos = {}
        cnt_e = {e: 0 for e in self.ENGS}
        for i, op in enumerate(ops):
            pos[i] = cnt_e[op.eng]
            cnt_e[op.eng] += 1

        def needs_wait(ci, di):
            op, dop = ops[ci], ops[di]
            if dop.dma_key is not None:
                return True
            if dop.fn is None:
                return False
            if dop.eng != op.eng:
                return True
            return op.eng != "pe" and (pos[ci] - pos[di]) <= SAME_ENG_WINDOW

        self._needs_wait = needs_wait
        for i, op in enumerate(ops):
            latest = {}
            for d in op.deps:
                dop = ops[d]
                if dop.dma_key is None and needs_wait(i, d):
                    if dop.eng not in latest or pos[d] > pos[latest[dop.eng]]:
                        latest[dop.eng] = d
            for d in latest.values():
                ops[d].mark = True
        cnt = {e: 0 for e in self.ENGS}
        for op in ops:
            if op.dma_key is None and op.mark:
                cnt[op.eng] += 1
                op.gidx = cnt[op.eng]
        eng_sems = {}
        for e in self.ENGS:
            n = (cnt[e] + SEM_CH - 1) // SEM_CH
            eng_sems[e] = [stack.enter_context(nc.semaphore(f"s_{e}_{i}")) for i in range(max(n, 1))]
        dma_sems = {k: stack.enter_context(nc.semaphore(f"d_{i}")) for i, k in enumerate(self.dma_count)}
        block = stack.enter_context(nc.Block())

        def make(engname):
            my = [(i, op) for i, op in enumerate(ops) if op.eng == engname]
            needs_wait = self._needs_wait

            def body(e):
                waited = {}
                for i, op in my:
                    emax = {}
                    for d in op.deps:
                        dop = ops[d]
                        if dop.dma_key is not None:
                            key = ("d", dop.dma_key)
                            val = 16 * dop.dma_ord
                            if waited.get(key, 0) < val:
                                emax[key] = max(emax.get(key, 0), val)
                        elif needs_wait(i, d) and dop.mark:
                            key = ("e", dop.eng)
                            if waited.get(key, 0) < dop.gidx:
                                emax[key] = max(emax.get(key, 0), dop.gidx)
                    for key, val in emax.items():
                        if key[0] == "d":
                            e.wait_ge(dma_sems[key[1]], val)
                        else:
                            e.wait_ge(eng_sems[key[1]][(val - 1) // SEM_CH], (val - 1) % SEM_CH + 1)
                        waited[key] = val
                        if TRACE is not None:
                            TRACE.append((engname, i, "wait", key, val))
                    if TRACE is not None:
                        TRACE.append((engname, i, "op", op.dma_key, op.dma_ord if op.dma_key is not None else (op.gidx if op.mark else 0)))
                    if op.fn is None:
                        continue
                    ins = op.fn(e)
                    if op.dma_key is not None:
                        ins.then_inc(dma_sems[op.dma_key], 16)
                    elif op.mark:
                        ins.then_inc(eng_sems[engname][(op.gidx - 1) // SEM_CH], 1)
            return body

        block.tensor(make("pe"))
        block.scalar(make("act"))
        block.vector(make("dve"))
        block.gpsimd(make("pool"))
        block.sync(make("sp"))


def build(nl, last):
    nc = bass.Bass("TRN2", target_bir_lowering=False)
    P = Prog()
    st = ExitStack()

    def din(name, shape):
        return nc.dram_tensor(name, shape, F32, kind="ExternalInput").ap()

    x_own = din("x_own", [TOK, D])
    x_halo = din("x_halo", [128, D])
    halo_flag = din("halo_flag", [128, 4])
    ident_d = din("ident", [128, 128])
    tril_d = din("trilT", [128, 128])
    biasT_d = din("biasT", [128, 2, 16, 128])
    maskT_d = din("maskT", [128, 2, 128])
    w_in = din("w_in", [nl, D, IN_W])
    w_out = din("w_out", [nl, D, D])
    w_gate = din("w_gate", [nl, D, DFF])
    w_up = din("w_up", [nl, D, DFF])
    w_down = din("w_down", [nl, DFF, D])
    sgu_wT_d = din("sgu_wT", [nl, 128, 16, 128])
    g1_d = din("g1", [nl, 128, DC])
    g2_d = din("g2", [nl, 128, DC])
    gmix_d = din("gmix", [nl, 128, DC])
    sgu_g_d = din("sgu_g", [nl, 1024])
    sgu_bT_d = din("sgu_bT", [nl, 128, 16])
    qg_d = din("qg", [nl, 64])
    kg_d = din("kg", [nl, 64])
    sinks_d = din("sinks", [nl, 16])
    y = nc.dram_tensor("y", [TOK, D], F32, kind="ExternalOutput").ap()

    def sb(name, shape, dtype):
        return st.enter_context(nc.sbuf_tensor("sb_" + name, shape, dtype))

    xT = sb("xT", [128, DC, TOK], F32)
    R = sb("R", [128, 16384], BF16)
    slabs = sb("slabs", [128, 4, 8192], BF16)
    bias8 = sb("bias8", [128, 2, 16, 128], BF16)
    stage = sb("stage", [128, 2048], F32)
    gateS = sb("gateS", [128, 4, 512], BF16)
    scrA = sb("scrA", [128, 1, 512], F32)
    xsq = sb("xsq", [128, 2, 512], BF16)
    zf = sb("zf", [128, 2, 512], F32)
    zn = sb("zn", [128, 2, 512], BF16)
    sq = sb("sq", [128, 512], F32)
    qT = sb("qT", [64, 8, 128], BF16)
    PT = sb("PT", [128, 2, 2, 512], BF16)
    wT = sb("wT", [128, 16, 128], BF16)
    sgbc = sb("sgbc", [128, 512], F32)
    ident_f = sb("ident_f", [128, 128], F32)
    ident_b = sb("ident_b", [128, 128], BF16)
    ones_b = sb("ones_b", [128, 128], BF16)
    tril_f = sb("tril_f", [128, 128], F32)
    g1 = sb("g1", [128, DC], F32)
    g2 = sb("g2", [128, DC], F32)
    gmix = sb("gmix", [128, DC], F32)
    sgub = sb("sgub", [128, 16], F32)
    qg = sb("qg", [128, 64], F32)
    kg = sb("kg", [128, 64], F32)
    esink = sb("esink", [128, 16], F32)
    stat = sb("stat", [128, 64], F32)
    hflag = sb("hflag", [128, 4], F32)
    epsT = sb("epsT", [128, 1], F32)

    ps = [st.enter_context(nc.psum_tensor(f"ps{i}", [128, 512], F32)) for i in range(8)]

    hT = R[:, 0:DC * 640].rearrange("p (c t) -> p c t", c=DC)
    mixT = R[:, 0:DC * 512].rearrange("p (c t) -> p c t", c=DC)
    h2T = R[:, 0:DC * 1024].rearrange("p (c t) -> p c t", c=DC)
    out_ab = slabs[:, 2, :].rearrange("p (b f) -> p b f", b=4)
    xTh = slabs[:, 2, 0:4096].bitcast(F32).rearrange("p (c t) -> p c t", c=DC)
    kT = slabs[:, 3, 0:9 * 512].rearrange("p (b k t) -> p b k t", b=9, k=4)
    vaug = slabs[:, 3, 9 * 512:9 * 512 + 9 * 4 * 66].rearrange("p (b k t) -> p b k t", b=9, k=4)
    aT = stage[:, :].bitcast(BF16).rearrange("p (j t) -> p j t", j=4)

    def slab_kn(i):
        return slabs[:, i, :].rearrange("p (k n) -> p k n", k=16)

    def slab_jn(i):
        return slabs[:, i, :].rearrange("p (j n) -> p j n", j=4)

    def psb(i):
        return ps[i][:, :].bitcast(BF16)

    add = P.add

    def pk(bank):
        return [("ps", 2, 0), ("ps", 2, 1)] if bank == 2 else [("ps", bank)]
    MIXK = [("mixT", s_, c_) for s_ in range(4) for c_ in range(DC)]

    add("sp", lambda e: e.dma_start(out=ident_f[:, :], in_=ident_d), writes=["ident_f"], dma_key="c0")
    add("sp", lambda e: e.dma_start(out=tril_f[:, :], in_=tril_d), writes=["tril_f"], dma_key="c1")
    mask_f = sq[:, 0:256].rearrange("p (a q) -> p a q", a=2)
    add("sp", lambda e: e.dma_start(out=mask_f, in_=maskT_d), writes=["sq"], dma_key="c2")
    add("sp", lambda e: e.dma_start(out=hflag[:, :], in_=halo_flag), writes=["hflag"], dma_key="c3")
    add("dve", lambda e: e.tensor_copy(out=ident_b[:, :], in_=ident_f[:, :]), reads=["ident_f"], writes=["ident_b"])
    add("dve", lambda e: e.memset(ones_b[:, :], 1.0), writes=["ones_b"])
    add("dve", lambda e: e.memset(epsT[:, :], EPS), writes=["epsT"])
    for pc in range(2):
        add("sp", lambda e, pc=pc: e.dma_start(out=stage[:, :], in_=biasT_d[:, pc].rearrange("p h q -> p (h q)")),
            writes=["stage"], dma_key="c4")
        add("dve", lambda e, pc=pc: e.scalar_tensor_tensor(
            out=bias8[:, pc], in0=stage[:, :].rearrange("p (h q) -> p h q", h=16), scalar=8.0,
            in1=mask_f[:, pc:pc + 1, :].to_broadcast([128, 16, 128]), op0=ALU.mult, op1=ALU.add),
            reads=["stage", "sq"], writes=["bias8"])

    slab_steps = []
    state = {"issued": 0, "cur": -1, "mixer_done": -1}
    step_layer = []
    buf_of = []

    def plan_layer(li):
        for seg in range(2):
            for cg in COL_GROUPS:
                slab_steps.append(("kn", w_in[li][:, cg * 512:(cg + 1) * 512], "m"))
            for ms in range(4):
                slab_steps.append(("kn", w_out[li][:, ms * 512:(ms + 1) * 512], "m"))
        for g in range(NG):
            slab_steps.append(("kn", w_gate[li][:, g * 512:(g + 1) * 512], "f"))
            slab_steps.append(("kn", w_up[li][:, g * 512:(g + 1) * 512], "f"))
            slab_steps.append(("jn", w_down[li][g * 512:(g + 1) * 512, :], "f"))

    for li in range(nl):
        n0 = len(slab_steps)
        plan_layer(li)
        step_layer.extend([li] * (len(slab_steps) - n0))
    mcount = 0
    fcount = 0
    for kind, src, ring in slab_steps:
        if ring == "m":
            buf_of.append(mcount % 2)
            mcount += 1
        else:
            buf_of.append([2, 3, 0, 1][fcount % 4])
            fcount += 1
    step_emitted = [False] * len(slab_steps)

    def pump(look=3):
        while state["issued"] < len(slab_steps) and state["issued"] <= state["cur"] + look:
            s = state["issued"]
            b = buf_of[s]
            prev = [t for t in range(s) if buf_of[t] == b]
            if prev and not step_emitted[prev[-1]]:
                break
            if slab_steps[s][2] == "f" and b in (2, 3) and step_layer[s] > state["mixer_done"]:
                break
            kind, src, ring = slab_steps[s]
            subs = [("slab", b, q_) for q_ in range(4)]
            if kind == "kn":
                add("pool", lambda e, b=b, src=src: e.dma_start(out=slab_kn(b), in_=src.rearrange("(k p) n -> p k n", p=128)),
                    writes=[("slab", b)], dma_key=("slab", b), after=subs)
            else:
                for q_ in range(4):
                    add("pool", lambda e, b=b, src=src, q_=q_: e.dma_start(
                        out=slab_jn(b)[:, :, q_ * 512:(q_ + 1) * 512],
                        in_=src[:, q_ * 512:(q_ + 1) * 512].rearrange("(j p) n -> p j n", p=128)),
                        writes=[("slab", b, q_)], dma_key=("slab", b), after=[("slab", b)])
            state["issued"] += 1

    def begin_step():
        state["cur"] += 1
        s = state["cur"]
        assert state["issued"] > s, "slab not issued"
        return buf_of[s]

    def end_step():
        step_emitted[state["cur"]] = True

    tin = {"n": 0}

    def transpose_in(src_rows, dst, dkey, tokn=128):
        si = tin["n"] % 2
        tin["n"] += 1
        xst = slabs[:, 3, si * 4096:(si + 1) * 4096].bitcast(F32)
        add("sp", lambda e: e.dma_start(out=xst, in_=src_rows), writes=[("xs", si)], dma_key=("xin", si))
        for q4 in range(4):
            bank = 3 + q4
            for j in range(4):
                c = q4 * 4 + j
                add("pe", lambda e, bank=bank, j=j, c=c: e.transpose(ps[bank][:, j * 128:(j + 1) * 128], xst[:, c * 128:(c + 1) * 128], ident_f[:, :]),
                    reads=[("xs", si), "ident_f"], writes=[("ps", bank)])
            eng = "act" if q4 % 2 == 0 else "dve"
            if eng == "act":
                add("act", lambda e, bank=bank, q4=q4: e.copy(out=dst[:, q4 * 4:(q4 + 1) * 4, :], in_=ps[bank][:, :].rearrange("p (j t) -> p j t", j=4)),
                    reads=[("ps", bank)], writes=[dkey])
            else:
                add("dve", lambda e, bank=bank, q4=q4: e.tensor_copy(out=dst[:, q4 * 4:(q4 + 1) * 4, :], in_=ps[bank][:, :].rearrange("p (j t) -> p j t", j=4)),
                    reads=[("ps", bank)], writes=[dkey])

    ncount = {"n": 0}

    def norm_tile(src, skeys, gain, dst, dkeys, n, after=()):
        r = 0
        bank = 0
        for c in range(DC):
            xs = c % 2
            add("act", lambda e, c=c, xs=xs: e.activation(out=xsq[:, xs, 0:n], in_=src[:, c, :], func=AF.Square),
                reads=list(skeys), writes=[("xsq", xs)])
            add("pe", lambda e, c=c, xs=xs: e.matmul(ps[bank][:, 0:n], ones_b[:, :], xsq[:, xs, 0:n], start=(c == 0), stop=(c == DC - 1)),
                reads=[("xsq", xs), "ones_b"], writes=[("ps", bank)])
        add("act", lambda e: e.activation(out=scrA[:, r, 0:n], in_=ps[bank][:, 0:n], func=AF.Sqrt, bias=epsT[:, 0:1], scale=1.0 / D),
            reads=[("ps", bank), "epsT"], writes=[("scrA", r)])
        add("dve", lambda e: e.reciprocal(out=scrA[:, r, 0:n], in_=scrA[:, r, 0:n]),
            reads=[("scrA", r)], writes=[("scrA", r)])
        for c in range(DC):
            add("dve", lambda e, c=c: e.scalar_tensor_tensor(out=dst[:, c, :], in0=src[:, c, :], scalar=gain[:, c:c + 1], in1=scrA[:, r, 0:n],
                                                             op0=ALU.mult, op1=ALU.mult),
                reads=list(skeys) + [("scrA", r), "gains"], writes=list(dkeys), after=after)

    def headnorm(src, nh, gbc, outs, okeys, skey):
        w = nh * 64
        add("dve", lambda e: e.tensor_tensor(out=sq[:, 0:w], in0=src, in1=src, op=ALU.mult), reads=[skey], writes=["sq"])
        add("dve", lambda e: e.tensor_reduce(out=stat[:, 0:nh], in_=sq[:, 0:w].rearrange("p (h d) -> p h d", h=nh), axis=AX.X, op=ALU.add),
            reads=["sq"], writes=["stat"])
        add("act", lambda e: e.activation(out=stat[:, 0:nh], in_=stat[:, 0:nh], func=AF.Sqrt, bias=epsT[:, 0:1], scale=1.0 / 64),
            reads=["stat", "epsT"], writes=["stat"])
        add("dve", lambda e: e.reciprocal(out=stat[:, 0:nh], in_=stat[:, 0:nh]),
            reads=["stat"], writes=["stat"])
        s3 = src.rearrange("p (h d) -> p h d", h=nh)
        add("dve", lambda e: e.tensor_tensor(out=s3, in0=s3, in1=stat[:, 0:nh].unsqueeze(2).to_broadcast([128, nh, 64]), op=ALU.mult),
            reads=[skey, "stat"], writes=[skey])
        for o in outs:
            add("dve", lambda e, o=o: e.tensor_tensor(out=o, in0=s3, in1=gbc, op=ALU.mult),
                reads=[skey, "lconst"], writes=list(okeys))

    def layer_consts(li):
        add("dve", lambda e: e.memset(vaug[:, :, :, 64:66], 1.0), writes=[("slab", 3)], after=[("xs", 0), ("xs", 1)])
        add("dve", lambda e: e.tensor_copy(out=vaug[:, 0, :, 64:65], in_=hflag[:, :].unsqueeze(2)),
            reads=["hflag"], writes=[("slab", 3)])
        add("sp", lambda e: e.dma_start(out=g1[:, :], in_=g1_d[li]), writes=["gains"], dma_key="lc0")
        add("sp", lambda e: e.dma_start(out=g2[:, :], in_=g2_d[li]), writes=["gains"], dma_key="lc1")
        add("sp", lambda e: e.dma_start(out=gmix[:, :], in_=gmix_d[li]), writes=["gains"], dma_key="lc2")
        add("sp", lambda e: e.dma_start(out=sgub[:, :], in_=sgu_bT_d[li]), writes=["lconst"], dma_key="lc3")
        add("sp", lambda e: e.dma_start(out=qg[:, :], in_=qg_d[li:li + 1, :].partition_broadcast(128)), writes=["lconst"], dma_key="lc4")
        add("sp", lambda e: e.dma_start(out=kg[:, :], in_=kg_d[li:li + 1, :].partition_broadcast(128)), writes=["lconst"], dma_key="lc5")
        add("sp", lambda e: e.dma_start(out=esink[:, :], in_=sinks_d[li:li + 1, :].partition_broadcast(128)), writes=["esink"], dma_key="lc6")
        add("act", lambda e: e.activation(out=esink[:, :], in_=esink[:, :], func=AF.Exp), reads=["esink"], writes=["esink"])
        add("sp", lambda e: e.dma_start(out=stage[:, :], in_=sgu_wT_d[li].rearrange("p h t -> p (h t)")), writes=["stage"], dma_key="lc7")
        add("dve", lambda e: e.tensor_tensor(out=wT[:, :, :], in0=stage[:, :].rearrange("p (h t) -> p h t", h=16),
                                             in1=tril_f[:, :].unsqueeze(1).to_broadcast([128, 16, 128]), op=ALU.mult),
            reads=["stage", "tril_f"], writes=["wT"])

    def mixer_segment(li, seg, first_layer):
        b0 = seg * 4
        tok0 = b0 * 128
        if seg == 0:
            if first_layer:
                norm_tile(xTh, [("slab", 2)], g1, hT[:, :, 0:128], [("hT", 0)], 128, after=MIXK)
        norm_tile(xT[:, :, tok0:tok0 + 512], [("xT", seg)], g1, hT[:, :, 128:640], [("hT", 1)], 512, after=MIXK)

        _dbg("norm1_%d" % seg)
        items = []
        for cg in COL_GROUPS:
            slots = list(range(0 if (seg == 0) else 1, 5)) if cg == 6 else list(range(1, 5))
            for sl in slots:
                items.append((cg, sl, sl == slots[0], sl == slots[-1]))

        if DEBUG_ITEMS is not None and seg == 1:
            items = items[:DEBUG_ITEMS]
        cur = {"buf": None}
        zp_i = {"n": 0}

        def stage_A(it):
            cg, sl, first, lastb = it
            if first:
                cur["buf"] = begin_step()
                pump()
            b = cur["buf"]
            attn = cg in (4, 5)
            if attn or zp_i.get("prev_attn", True):
                bank = 0
            else:
                bank = 1 - zp_i["prev_bank"]
            zp_i["prev_attn"] = attn
            zp_i["prev_bank"] = bank
            W = slab_kn(b)
            hk = ("hT", 0 if sl == 0 else 1)
            for k in range(DC):
                add("pe", lambda e, k=k, bank=bank, W=W, sl=sl: e.matmul(ps[bank][:, :], hT[:, k, sl * 128:(sl + 1) * 128], W[:, k, :], start=(k == 0), stop=(k == DC - 1)),
                    reads=[hk, ("slab", b)], writes=[("ps", bank)])
            if lastb:
                end_step()
            return bank

        def stage_B(it, bank, idx):
            cg, sl, first, lastb = it
            blk = b0 + sl - 1
            kidx = blk + 1
            r = idx % 2
            zk = ("zf", r)
            if cg == 6:
                add("act", lambda e: e.copy(out=zf[:, r, 0:256], in_=ps[bank][:, 0:256]), reads=[("ps", bank)], writes=[zk])
                add("act", lambda e: e.copy(out=vaug[:, kidx, :, 0:64], in_=ps[bank][:, 256:512].rearrange("p (k d) -> p k d", k=4)),
                    reads=[("ps", bank)], writes=[("slab", 3)])
                yield
                headnorm(zf[:, r, 0:256], 4, kg[:, :].unsqueeze(1).to_broadcast([128, 4, 64]),
                         [zn[:, r, 0:256].rearrange("p (k d) -> p k d", k=4)], [("zn", r)], zk)
                for kv in range(4):
                    add("pe", lambda e, kv=kv: e.transpose(psb(2)[0:64, r * 512 + kv * 128:r * 512 + (kv + 1) * 128], zn[:, r, kv * 64:(kv + 1) * 64], ident_b[:, :]),
                        reads=[("zn", r), "ident_b"], writes=[("ps", 2, 0), ("ps", 2, 1)])
                add("act", lambda e: e.copy(out=kT[0:64, kidx], in_=psb(2)[0:64, r * 512:(r + 1) * 512].rearrange("p (k t) -> p k t", k=4)),
                    reads=[("ps", 2, 0), ("ps", 2, 1)], writes=[("slab", 3)])
            elif cg in (4, 5):
                qh = cg - 4
                add("act", lambda e: e.copy(out=zf[:, r, :], in_=ps[bank][:, :]), reads=[("ps", bank)], writes=[zk])
                yield
                headnorm(zf[:, r, :], 8, qg[:, :].unsqueeze(1).to_broadcast([128, 8, 64]),
                         [zn[:, r, :].rearrange("p (h d) -> p h d", h=8)], [("zn", r)], zk)
                for hl in range(8):
                    add("pe", lambda e, hl=hl: e.transpose(psb(2)[0:64, hl * 128:(hl + 1) * 128], zn[:, r, hl * 64:(hl + 1) * 64], ident_b[:, :]),
                        reads=[("zn", r), "ident_b"], writes=[("ps", 2, 0), ("ps", 2, 1)])
                add("dve", lambda e: e.tensor_copy(out=qT[0:64, :, :], in_=psb(2)[0:64, :].rearrange("p (j t) -> p j t", j=8)),
                    reads=[("ps", 2, 0), ("ps", 2, 1)], writes=["qT"])
                for kvl in range(2):
                    kv = 2 * qh + kvl
                    for pc in range(2):
                        bk = 3 + 2 * kvl + pc
                        ki = kidx - 1 + pc
                        Sv = ps[bk][:, :].rearrange("p (g t) -> p g t", g=4)
                        add("pe", lambda e, bk=bk, pc=pc, kv=kv: e.matmul(ps[bk][:, :], ident_b[:, :], bias8[:, pc, 4 * kv:4 * kv + 4, :].rearrange("p g q -> p (g q)"), start=True, stop=False),
                            reads=["ident_b", "bias8"], writes=[("ps", bk)])
                        add("pe", lambda e, bk=bk, ki=ki, kv=kv, kvl=kvl: e.matmul(
                            ps[bk][:, :], kT[0:64, ki, kv, :], qT[0:64, 4 * kvl:4 * kvl + 4, :].rearrange("p j t -> p (j t)"),
                            start=False, stop=True),
                            reads=[("slab", 3), "qT"], writes=[("ps", bk)])
                        add("act", lambda e, bk=bk, kvl=kvl, pc=pc: e.activation(out=PT[:, kvl, pc, :], in_=ps[bk][:, :], func=AF.Exp, scale=0.125),
                            reads=[("ps", bk)], writes=[("PT", kvl, pc)])
                for kvl in range(2):
                    kv = 2 * qh + kvl
                    ob = 1 if kvl == 0 else 7
                    Ov = ps[ob][:, 0:264].rearrange("p (g d) -> p g d", g=4)
                    for g in range(4):
                        for pc in range(2):
                            ki = kidx - 1 + pc
                            add("pe", lambda e, Ov=Ov, g=g, pc=pc, ki=ki, kv=kv, kvl=kvl: e.matmul(
                                Ov[:, g, 0:65], PT[:, kvl, pc, g * 128:(g + 1) * 128], vaug[:, ki, kv, 0:65], start=(pc == 0), stop=(pc == 1)),
                                reads=[("PT", kvl, pc), ("slab", 3)], writes=[("ps", ob)])
                    sc = 16 + 4 * kvl
                    add("dve", lambda e, Ov=Ov, sc=sc, kv=kv: e.tensor_tensor(out=stat[:, sc:sc + 4], in0=Ov[:, :, 64:65].rearrange("p g o -> p (g o)"), in1=esink[:, 4 * kv:4 * kv + 4], op=ALU.add),
                        reads=[("ps", ob), "esink"], writes=[("stat2", kvl)])
                    add("dve", lambda e, sc=sc: e.reciprocal(out=stat[:, sc:sc + 4], in_=stat[:, sc:sc + 4]), reads=[("stat2", kvl)], writes=[("stat2", kvl)])
                    add("dve", lambda e, Ov=Ov, sc=sc, kv=kv: e.tensor_tensor(
                        out=out_ab[:, sl - 1, 1024 + kv * 256:1024 + (kv + 1) * 256].rearrange("p (g d) -> p g d", g=4),
                        in0=Ov[:, :, 0:64], in1=stat[:, sc:sc + 4].unsqueeze(2).to_broadcast([128, 4, 64]), op=ALU.mult),
                        reads=[("ps", ob), ("stat2", kvl)], writes=[("slab", 2)])
            elif cg in (2, 3):
                vh = cg - 2
                if first:
                    add("sp", lambda e: e.dma_start(out=sgbc[:, :], in_=sgu_g_d[li:li + 1, vh * 512:(vh + 1) * 512].partition_broadcast(128)),
                        writes=["sgbc"], dma_key="sgbc")
                add("act", lambda e: e.activation(out=zf[:, r, :], in_=ps[bank][:, :], func=AF.Gelu), reads=[("ps", bank)], writes=[zk])
                yield
                w = 512
                add("dve", lambda e: e.tensor_tensor(out=sq[:, 0:w], in0=zf[:, r, :], in1=zf[:, r, :], op=ALU.mult), reads=[zk], writes=["sq"])
                add("dve", lambda e: e.tensor_reduce(out=stat[:, 0:8], in_=sq[:, 0:w].rearrange("p (h d) -> p h d", h=8), axis=AX.X, op=ALU.add),
                    reads=["sq"], writes=["stat"])
                add("act", lambda e: e.activation(out=stat[:, 0:8], in_=stat[:, 0:8], func=AF.Sqrt, bias=epsT[:, 0:1], scale=1.0 / 64),
                    reads=["stat", "epsT"], writes=["stat"])
                add("dve", lambda e: e.reciprocal(out=stat[:, 0:8], in_=stat[:, 0:8]),
                    reads=["stat"], writes=["stat"])
                z3 = zf[:, r, :].rearrange("p (h d) -> p h d", h=8)
                add("dve", lambda e: e.tensor_tensor(out=z3, in0=z3, in1=stat[:, 0:8].unsqueeze(2).to_broadcast([128, 8, 64]), op=ALU.mult),
                    reads=[zk, "stat"], writes=[zk])
                add("dve", lambda e: e.tensor_tensor(out=zn[:, r, :], in0=zf[:, r, :], in1=sgbc[:, :], op=ALU.mult),
                    reads=[zk, "sgbc"], writes=[("zn", r)])
                gb = 3 + (idx % 2)
                for h in range(8):
                    add("pe", lambda e, h=h, gb=gb: e.matmul(ps[gb][:, h * 64:(h + 1) * 64], wT[:, vh * 8 + h, :], zn[:, r, h * 64:(h + 1) * 64], start=True, stop=True),
                        reads=["wT", ("zn", r)], writes=[("ps", gb)])
                add("dve", lambda e, gb=gb: e.tensor_tensor(out=gateS[:, sl - 1, :].rearrange("p (h d) -> p h d", h=8),
                                                             in0=ps[gb][:, :].rearrange("p (h d) -> p h d", h=8),
                                                             in1=sgub[:, vh * 8:(vh + 1) * 8].unsqueeze(2).to_broadcast([128, 8, 64]), op=ALU.add),
                    reads=[("ps", gb), "lconst"], writes=[("gateS", sl - 1)])
            else:
                uh = cg
                add("act", lambda e: e.activation(out=zf[:, r, :], in_=ps[bank][:, :], func=AF.Gelu), reads=[("ps", bank)], writes=[zk])
                yield
                add("dve", lambda e: e.tensor_tensor(out=out_ab[:, sl - 1, uh * 512:(uh + 1) * 512], in0=zf[:, r, :], in1=gateS[:, sl - 1, :], op=ALU.mult),
                    reads=[zk, ("gateS", sl - 1)], writes=[("slab", 2)])

        pump(look=1)
        banks = {}
        banks[0] = stage_A(items[0])
        for i in range(len(items)):
            gen = stage_B(items[i], banks[i], i)
            next(gen)
            if i + 1 < len(items):
                banks[i + 1] = stage_A(items[i + 1])
            for _ in gen:
                pass

        _dbg("inproj%d" % seg)
        for sl in range(4):
            for half in range(2):
                for pc_ in range(2):
                    lo_ = half * 1024 + pc_ * 512
                    add("dve", lambda e, sl=sl, lo_=lo_: e.tensor_tensor(out=sq[:, :], in0=out_ab[:, sl, lo_:lo_ + 512], in1=out_ab[:, sl, lo_:lo_ + 512], op=ALU.mult),
                        reads=[("slab", 2)], writes=["sq"])
                    add("dve", lambda e, half=half, pc_=pc_: e.tensor_reduce(out=stat[:, 36 + 2 * half + pc_:37 + 2 * half + pc_], in_=sq[:, :], axis=AX.X, op=ALU.add),
                        reads=["sq"], writes=[("stat4", half, pc_)])
            if DEBUG_ON < 1:
                continue
            add("dve", lambda e: e.tensor_reduce(out=stat[:, 32:34], in_=stat[:, 36:40].rearrange("p (a b) -> p a b", a=2), axis=AX.X, op=ALU.add),
                reads=[("stat4", 0, 0), ("stat4", 0, 1), ("stat4", 1, 0), ("stat4", 1, 1)], writes=[("stat3", 0), ("stat3", 1)])
            add("act", lambda e: e.activation(out=stat[:, 32:34], in_=stat[:, 32:34], func=AF.Sqrt, bias=epsT[:, 0:1], scale=1.0 / 1024),
                reads=[("stat3", 0), ("stat3", 1), "epsT"], writes=[("stat3", 0), ("stat3", 1)])
            add("dve", lambda e: e.reciprocal(out=stat[:, 32:34], in_=stat[:, 32:34]),
                reads=[("stat3", 0), ("stat3", 1)], writes=[("stat3", 0), ("stat3", 1)])
            if DEBUG_ON < 2:
                continue
            for half in range(2):
                add("dve", lambda e, sl=sl, half=half: e.tensor_tensor(out=out_ab[:, sl, half * 1024:(half + 1) * 1024], in0=out_ab[:, sl, half * 1024:(half + 1) * 1024],
                                                                        in1=stat[:, 32 + half:33 + half].to_broadcast([128, 1024]), op=ALU.mult),
                    reads=[("slab", 2), ("stat3", half)], writes=[("slab", 2)])
            if DEBUG_ON < 3:
                continue
            for hb in range(2):
                bank = 3 + 2 * (sl % 2) + hb
                for j in range(8):
                    c = hb * 8 + j
                    add("pe", lambda e, bank=bank, j=j, c=c, sl=sl: e.transpose(psb(bank)[:, j * 128:(j + 1) * 128], out_ab[:, sl, c * 128:(c + 1) * 128], ident_b[:, :]),
                        reads=[("slab", 2), "ident_b"], writes=[("ps", bank)])
                if DEBUG_ON < 4:
                    continue
                add("dve", lambda e, bank=bank, hb=hb, sl=sl: e.tensor_tensor(
                    out=mixT[:, hb * 8:(hb + 1) * 8, sl * 128:(sl + 1) * 128], in0=psb(bank)[:, :].rearrange("p (j t) -> p j t", j=8),
                    in1=gmix[:, hb * 8:(hb + 1) * 8].unsqueeze(2).to_broadcast([128, 8, 128]), op=ALU.mult),
                    reads=[("ps", bank), "gains"], writes=[("mixT", sl, hb * 8 + j_) for j_ in range(8)], after=[("hT", 0), ("hT", 1)])
        if seg == 1:
            state["mixer_done"] = li

        _dbg("outnorm%d" % seg)
        wb = [0, 1, 7]
        wi = 0
        for ms in range(4):
            b = begin_step()
            pump()
            W = slab_kn(b)
            for mi in range(4):
                m = ms * 4 + mi
                bank = wb[wi % 3]
                wi += 1
                for k in range(DC):
                    add("pe", lambda e, k=k, bank=bank, W=W, mi=mi: e.matmul(ps[bank][:, :], W[:, k, mi * 128:(mi + 1) * 128], mixT[:, k, :], start=(k == 0), stop=(k == DC - 1)),
                        reads=[("mixT", s_, k) for s_ in range(4)] + [("slab", b)], writes=[("ps", bank)])
                add("dve", lambda e, bank=bank, m=m: e.tensor_tensor(out=xT[:, m, tok0:tok0 + 512], in0=ps[bank][:, :], in1=xT[:, m, tok0:tok0 + 512], op=ALU.add),
                    reads=[("ps", bank), ("xT", seg)], writes=[("xT", seg)])
            end_step()

    def ffn(li):
        for t in range(2):
            norm_tile(xT[:, :, t * 512:(t + 1) * 512], [("xT", t)], g2, h2T[:, :, t * 512:(t + 1) * 512], [("h2T", t)], 512, after=MIXK + [("hT", 0), ("hT", 1)])
        gi = 0
        for g in range(NG):
            bg = begin_step()
            pump()
            bu = begin_step()
            pump()
            Wg = slab_kn(bg)
            Wu = slab_kn(bu)
            for t in range(2):
                for j in range(4):
                    gb = 1 + (gi % 2)
                    ub = 3 + (gi % 2)
                    r = gi % 2
                    gi += 1
                    for k in range(DC):
                        add("pe", lambda e, k=k, gb=gb, j=j, t=t, Wg=Wg: e.matmul(ps[gb][:, :], Wg[:, k, j * 128:(j + 1) * 128], h2T[:, k, t * 512:(t + 1) * 512], start=(k == 0), stop=(k == DC - 1)),
                            reads=[("h2T", t), ("slab", bg)], writes=pk(gb))
                    for k in range(DC):
                        add("pe", lambda e, k=k, ub=ub, j=j, t=t, Wu=Wu: e.matmul(ps[ub][:, :], Wu[:, k, j * 128:(j + 1) * 128], h2T[:, k, t * 512:(t + 1) * 512], start=(k == 0), stop=(k == DC - 1)),
                            reads=[("h2T", t), ("slab", bu)], writes=[("ps", ub)])
                    add("act", lambda e, gb=gb, r=r: e.activation(out=zf[:, r, :], in_=ps[gb][:, :], func=AF.Silu), reads=pk(gb), writes=[("zf", r)])
                    add("dve", lambda e, ub=ub, r=r, j=j, t=t: e.tensor_tensor(out=aT[:, j, t * 512:(t + 1) * 512], in0=ps[ub][:, :], in1=zf[:, r, :], op=ALU.mult),
                        reads=[("ps", ub), ("zf", r)], writes=["stage"])
            step_emitted[state["cur"] - 1] = True
            end_step()
            bd = begin_step()
            pump()
            Wd = slab_jn(bd)
            di = 0
            for t in range(2):
                for m in range(DC):
                    db = 5 + (di % 3)
                    di += 1
                    for j in range(4):
                        add("pe", lambda e, db=db, j=j, m=m, t=t, Wd=Wd: e.matmul(ps[db][:, :], Wd[:, j, m * 128:(m + 1) * 128], aT[:, j, t * 512:(t + 1) * 512], start=(j == 0), stop=(j == 3)),
                            reads=["stage", ("slab", bd)] + [("slab", bd, q_) for q_ in range(4)], writes=[("ps", db)])
                    add("dve", lambda e, db=db, m=m, t=t: e.tensor_tensor(out=xT[:, m, t * 512:(t + 1) * 512], in0=ps[db][:, :], in1=xT[:, m, t * 512:(t + 1) * 512], op=ALU.add),
                        reads=[("ps", db), ("xT", t)], writes=[("xT", t)])
            end_step()

    try:
        transpose_in(x_halo, xTh, ("slab", 2))
        for b in range(NB):
            transpose_in(x_own[b * 128:(b + 1) * 128, :], xT[:, :, b * 128:(b + 1) * 128], ("xT", b // 4))
        _dbg("load")
        for li in range(nl):
            layer_consts(li)
            for seg in range(2):
                mixer_segment(li, seg, li == 0)
                _dbg("seg%d" % seg)
            P.barrier()
            _dbg("mixer")
            ffn(li)
            P.barrier()
    except _Stop:
        pass
    oi = 0
    for b in range(NB):
        ost = slabs[:, b // 2, (b % 2) * 4096:(b % 2 + 1) * 4096].bitcast(F32)
        for q4 in range(4):
            bank = 3 + (oi % 4)
            oi += 1
            for j in range(4):
                c = q4 * 4 + j
                add("pe", lambda e, bank=bank, j=j, c=c, b=b: e.transpose(ps[bank][:, j * 128:(j + 1) * 128], xT[:, c, b * 128:(b + 1) * 128], ident_f[:, :]),
                    reads=[("xT", b // 4), "ident_f"], writes=[("ps", bank)])
            oa = [("slab", b // 2)] + [("slab", b // 2, q_) for q_ in range(4)]
            if q4 % 2 == 0:
                add("act", lambda e, bank=bank, q4=q4, ost=ost: e.copy(out=ost[:, q4 * 512:(q4 + 1) * 512], in_=ps[bank][:, :]),
                    reads=[("ps", bank)], writes=[("ost", b, q4)], after=oa)
            else:
                add("dve", lambda e, bank=bank, q4=q4, ost=ost: e.tensor_copy(out=ost[:, q4 * 512:(q4 + 1) * 512], in_=ps[bank][:, :]),
                    reads=[("ps", bank)], writes=[("ost", b, q4)], after=oa)
        add("sp", lambda e, b=b, ost=ost: e.dma_start(out=y[b * 128:(b + 1) * 128, :], in_=ost),
            reads=[("ost", b, q_) for q_ in range(4)], writes=[("y", b)], dma_key="yout")
    add("sp", None, reads=[("y", b) for b in range(NB)], writes=["y_done"])
    P.emit(nc, st)
    st.close()
    return nc


def _bucket_table():
    q = np.arange(128)[:, None]
    k = np.arange(256)[None, :]
    dist = q + 128 - k
    n = np.maximum(dist, 0)
    max_exact = 16
    nf = np.maximum(n, 1).astype(np.float32)
    large = max_exact + (np.log(nf / max_exact) / np.log(128 / max_exact) * (32 - max_exact)).astype(np.int32)
    large = np.minimum(large, 31)
    bucket = np.where(n < max_exact, n, large)
    valid = (dist >= 0) & (dist < 128)
    return bucket, valid


_NC_CACHE = {}


def _get_nc(nl):
    if nl not in _NC_CACHE:
        _NC_CACHE[nl] = build(nl, True)
    return _NC_CACHE[nl]


def kernel(x, rel_bias, norm1_g, w_in, sgu_norm_g, sgu_w, sgu_b, q_norm_g, k_norm_g, sinks,
           out_norm_a, out_norm_b, w_out, norm2_g, w_gate, w_up, w_down):
    f32 = np.float32
    x = np.asarray(x, f32)
    L = w_in.shape[0]
    bucket, valid = _bucket_table()
    bias_g = np.asarray(rel_bias, f32)[bucket]
    biasT = np.ascontiguousarray(bias_g.reshape(128, 2, 128, 16).transpose(2, 1, 3, 0))
    maskT = np.ascontiguousarray(np.where(valid, 0.0, NEG).astype(f32).reshape(128, 2, 128).transpose(2, 1, 0))
    trilT = np.ascontiguousarray(np.triu(np.ones((128, 128), f32)))
    ident = np.eye(128, dtype=f32)

    def chunked(v):
        return np.ascontiguousarray(np.asarray(v, f32).reshape(L, DC, 128).transpose(0, 2, 1))

    g1 = chunked(norm1_g)
    g2 = chunked(norm2_g)
    gmix = chunked(np.concatenate([np.asarray(out_norm_a, f32), np.asarray(out_norm_b, f32)], axis=1))
    sgu_wT = np.ascontiguousarray(np.asarray(sgu_w, f32).transpose(0, 3, 1, 2))
    sgu_bT = np.ascontiguousarray(np.asarray(sgu_b, f32).transpose(0, 2, 1))
    sgu_g = np.ascontiguousarray(np.asarray(sgu_norm_g, f32).reshape(L, 1024))

    xs = x.reshape(SEQ, D)
    nl = 1
    nc = _get_nc(nl)
    for l in range(L):
        in_maps = []
        for c in range(NCORE):
            own = xs[c * TOK:(c + 1) * TOK]
            halo = xs[c * TOK - 128:c * TOK] if c > 0 else np.zeros((128, D), f32)
            flag = np.full((128, 4), 1.0 if c > 0 else 0.0, f32)
            in_maps.append({
                "x_own": np.ascontiguousarray(own), "x_halo": np.ascontiguousarray(halo), "halo_flag": flag,
                "ident": ident, "trilT": trilT, "biasT": biasT, "maskT": maskT,
                "w_in": np.asarray(w_in[l:l + 1], f32), "w_out": np.asarray(w_out[l:l + 1], f32),
                "w_gate": np.asarray(w_gate[l:l + 1], f32), "w_up": np.asarray(w_up[l:l + 1], f32),
                "w_down": np.asarray(w_down[l:l + 1], f32),
                "sgu_wT": sgu_wT[l:l + 1], "g1": g1[l:l + 1], "g2": g2[l:l + 1], "gmix": gmix[l:l + 1],
                "sgu_g": sgu_g[l:l + 1], "sgu_bT": sgu_bT[l:l + 1],
                "qg": np.asarray(q_norm_g[l:l + 1], f32), "kg": np.asarray(k_norm_g[l:l + 1], f32),
                "sinks": np.asarray(sinks[l:l + 1], f32),
            })
        res = run_bass_kernel_spmd(nc, in_maps, core_ids=list(range(NCORE)))
        xs = np.concatenate([np.asarray(r["y"], f32) for r in res.results], axis=0)
    return xs.reshape(1, SEQ, D)
```

```python
import numpy as np
from contextlib import ExitStack
import concourse.bass as bass
import concourse.mybir as mybir
from concourse.bass_utils import run_bass_kernel_spmd

F32 = mybir.dt.float32
BF16 = mybir.dt.bfloat16
AF = mybir.ActivationFunctionType
ALU = mybir.AluOpType
AX = mybir.AxisListType

D = 2048
SEQ = 8192
NCORE = 8
TOK = SEQ // NCORE
NB = TOK // 128
DC = D // 128
IN_W = 3584
DFF = 5632
NG = DFF // 512
EPS = 1e-6
NEG = -240000.0
SEM_CH = 1024
SAME_ENG_WINDOW = 3
COL_GROUPS = [6, 4, 5, 2, 0, 3, 1]


DEBUG_STOP = None
DEBUG_ITEMS = None
DEBUG_ON = 99
TRACE = None


class _Stop(Exception):
    pass


def _dbg(name):
    if DEBUG_STOP == name:
        raise _Stop()


class _Op:
    __slots__ = ("eng", "fn", "deps", "dma_key", "dma_ord", "mark", "gidx")


class Prog:
    ENGS = ("pe", "act", "dve", "pool", "sp")

    def __init__(self):
        self.ops = []
        self.last_w = {}
        self.readers = {}
        self.dma_count = {}
        self.barrier_deps = set()
        self.last_on_eng = {}
        self.last_dma = {}

    def add(self, eng, fn, reads=(), writes=(), dma_key=None, after=()):
        idx = len(self.ops)
        deps = set(self.barrier_deps)
        for a in after:
            w = self.last_w.get(a)
            if w is not None:
                deps.add(w)
            deps |= self.readers.get(a, set())
        for r in reads:
            w = self.last_w.get(r)
            if w is not None:
                deps.add(w)
        for w_ in writes:
            w = self.last_w.get(w_)
            if w is not None:
                deps.add(w)
            deps |= self.readers.get(w_, set())
        for r in reads:
            self.readers.setdefault(r, set()).add(idx)
        for w_ in writes:
            self.last_w[w_] = idx
            self.readers[w_] = set()
        op = _Op()
        op.eng = eng
        op.fn = fn
        op.deps = deps
        op.dma_key = dma_key
        op.mark = False
        op.gidx = 0
        op.dma_ord = 0
        if dma_key is not None:
            self.dma_count[dma_key] = self.dma_count.get(dma_key, 0) + 1
            op.dma_ord = self.dma_count[dma_key]
            self.last_dma[dma_key] = idx
        else:
            self.last_on_eng[eng] = idx
        self.ops.append(op)
        return idx

    def barrier(self):
        self.barrier_deps = set(self.last_on_eng.values()) | set(self.last_dma.values())

    def emit(self, nc, stack):
        ops = self.ops
        pos = {}
        cnt_e = {e: 0 for e in self.ENGS}
        for i, op in enumerate(ops):
            pos[i] = cnt_e[op.eng]
            cnt_e[op.eng] += 1

        def needs_wait(ci, di):
            op, dop = ops[ci], ops[di]
            if dop.dma_key is not None:
                return True
            if dop.fn is None:
                return False
            if dop.eng != op.eng:
                return True
            return op.eng != "pe" and (pos[ci] - pos[di]) <= SAME_ENG_WINDOW

        self._needs_wait = needs_wait
        for i, op in enumerate(ops):
            latest = {}
            for d in op.deps:
                dop = ops[d]
                if dop.dma_key is None and needs_wait(i, d):
                    if dop.eng not in latest or pos[d] > pos[latest[dop.eng]]:
                        latest[dop.eng] = d
            for d in latest.values():
                ops[d].mark = True
        cnt = {e: 0 for e in self.ENGS}
        for op in ops:
            if op.dma_key is None and op.mark:
                cnt[op.eng] += 1
                op.gidx = cnt[op.eng]
        eng_sems = {}
        for e in self.ENGS:
            n = (cnt[e] + SEM_CH - 1) // SEM_CH
            eng_sems[e] = [stack.enter_context(nc.semaphore(f"s_{e}_{i}")) for i in range(max(n, 1))]
        dma_sems = {k: stack.enter_context(nc.semaphore(f"d_{i}")) for i, k in enumerate(self.dma_count)}
        block = stack.enter_context(nc.Block())

        def make(engname):
            my = [(i, op) for i, op in enumerate(ops) if op.eng == engname]
            needs_wait = self._needs_wait

            def body(e):
                waited = {}
                for i, op in my:
                    emax = {}
                    for d in op.deps:
                        dop = ops[d]
                        if dop.dma_key is not None:
                            key = ("d", dop.dma_key)
                            val = 16 * dop.dma_ord
                            if waited.get(key, 0) < val:
                                emax[key] = max(emax.get(key, 0), val)
                        elif needs_wait(i, d) and dop.mark:
                            key = ("e", dop.eng)
                            if waited.get(key, 0) < dop.gidx:
                                emax[key] = max(emax.get(key, 0), dop.gidx)
                    for key, val in emax.items():
                        if key[0] == "d":
                            e.wait_ge(dma_sems[key[1]], val)
                        else:
                            e.wait_ge(eng_sems[key[1]][(val - 1) // SEM_CH], (val - 1) % SEM_CH + 1)
                        waited[key] = val
                        if TRACE is not None:
                            TRACE.append((engname, i, "wait", key, val))
                    if TRACE is not None:
                        TRACE.append((engname, i, "op", op.dma_key, op.dma_ord if op.dma_key is not None else (op.gidx if op.mark else 0)))
                    if op.fn is None:
                        continue
                    ins = op.fn(e)
                    if op.dma_key is not None:
                        ins.then_inc(dma_sems[op.dma_key], 16)
                    elif op.mark:
                        ins.then_inc(eng_sems[engname][(op.gidx - 1) // SEM_CH], 1)
            return body

        block.tensor(make("pe"))
        block.scalar(make("act"))
        block.vector(make("dve"))
        block.gpsimd(make("pool"))
        block.sync(make("sp"))


def build(nl, last):
    nc = bass.Bass("TRN2", target_bir_lowering=False)
    P = Prog()
    st = ExitStack()

    def din(name, shape):
        return nc.dram_tensor(name, shape, F32, kind="ExternalInput").ap()

    x_own = din("x_own", [TOK, D])
    x_halo = din("x_halo", [128, D])
    halo_flag = din("halo_flag", [128, 4])
    ident_d = din("ident", [128, 128])
    tril_d = din("trilT", [128, 128])
    biasT_d = din("biasT", [128, 2, 16, 128])
    maskT_d = din("maskT", [128, 2, 128])
    w_in = din("w_in", [nl, D, IN_W])
    w_out = din("w_out", [nl, D, D])
    w_gate = din("w_gate", [nl, D, DFF])
    w_up = din("w_up", [nl, D, DFF])
    w_down = din("w_down", [nl, DFF, D])
    sgu_wT_d = din("sgu_wT", [nl, 128, 16, 128])
    g1_d = din("g1", [nl, 128, DC])
    g2_d = din("g2", [nl, 128, DC])
    gmix_d = din("gmix", [nl, 128, DC])
    sgu_g_d = din("sgu_g", [nl, 1024])
    sgu_bT_d = din("sgu_bT", [nl, 128, 16])
    qg_d = din("qg", [nl, 64])
    kg_d = din("kg", [nl, 64])
    sinks_d = din("sinks", [nl, 16])
    y = nc.dram_tensor("y", [TOK, D], F32, kind="ExternalOutput").ap()

    def sb(name, shape, dtype):
        return st.enter_context(nc.sbuf_tensor("sb_" + name, shape, dtype))

    xT = sb("xT", [128, DC, TOK], F32)
    R = sb("R", [128, 16384], BF16)
    slabs = sb("slabs", [128, 4, 8192], BF16)
    bias8 = sb("bias8", [128, 2, 16, 128], BF16)
    stage = sb("stage", [128, 2048], F32)
    gateS = sb("gateS", [128, 4, 512], BF16)
    scrA = sb("scrA", [128, 1, 512], F32)
    xsq = sb("xsq", [128, 2, 512], BF16)
    zf = sb("zf", [128, 2, 512], F32)
    zn = sb("zn", [128, 2, 512], BF16)
    sq = sb("sq", [128, 512], F32)
    qT = sb("qT", [64, 8, 128], BF16)
    PT = sb("PT", [128, 2, 2, 512], BF16)
    wT = sb("wT", [128, 16, 128], BF16)
    sgbc = sb("sgbc", [128, 512], F32)
    ident_f = sb("ident_f", [128, 128], F32)
    ident_b = sb("ident_b", [128, 128], BF16)
    ones_b = sb("ones_b", [128, 128], BF16)
    tril_f = sb("tril_f", [128, 128], F32)
    g1 = sb("g1", [128, DC], F32)
    g2 = sb("g2", [128, DC], F32)
    gmix = sb("gmix", [128, DC], F32)
    sgub = sb("sgub", [128, 16], F32)
    qg = sb("qg", [128, 64], F32)
    kg = sb("kg", [128, 64], F32)
    esink = sb("esink", [128, 16], F32)
    stat = sb("stat", [128, 64], F32)
    hflag = sb("hflag", [128, 4], F32)
    epsT = sb("epsT", [128, 1], F32)

    ps = [st.enter_context(nc.psum_tensor(f"ps{i}", [128, 512], F32)) for i in range(8)]

    hT = R[:, 0:DC * 640].rearrange("p (c t) -> p c t", c=DC)
    mixT = R[:, 0:DC * 512].rearrange("p (c t) -> p c t", c=DC)
    h2T = R[:, 0:DC * 1024].rearrange("p (c t) -> p c t", c=DC)
    out_ab = slabs[:, 2, :].rearrange("p (b f) -> p b f", b=4)
    xTh = slabs[:, 2, 0:4096].bitcast(F32).rearrange("p (c t) -> p c t", c=DC)
    kT = slabs[:, 3, 0:9 * 512].rearrange("p (b k t) -> p b k t", b=9, k=4)
    vaug = slabs[:, 3, 9 * 512:9 * 512 + 9 * 4 * 66].rearrange("p (b k t) -> p b k t", b=9, k=4)
    aT = stage[:, :].bitcast(BF16).rearrange("p (j t) -> p j t", j=4)

    def slab_kn(i):
        return slabs[:, i, :].rearrange("p (k n) -> p k n", k=16)

    def slab_jn(i):
        return slabs[:, i, :].rearrange("p (j n) -> p j n", j=4)

    def psb(i):
        return ps[i][:, :].bitcast(BF16)

    add = P.add

    def pk(bank):
        return [("ps", 2, 0), ("ps", 2, 1)] if bank == 2 else [("ps", bank)]
    MIXK = [("mixT", s_, c_) for s_ in range(4) for c_ in range(DC)]

    add("sp", lambda e: e.dma_start(out=ident_f[:, :], in_=ident_d), writes=["ident_f"], dma_key="c0")
    add("sp", lambda e: e.dma_start(out=tril_f[:, :], in_=tril_d), writes=["tril_f"], dma_key="c1")
    mask_f = sq[:, 0:256].rearrange("p (a q) -> p a q", a=2)
    add("sp", lambda e: e.dma_start(out=mask_f, in_=maskT_d), writes=["sq"], dma_key="c2")
    add("sp", lambda e: e.dma_start(out=hflag[:, :], in_=halo_flag), writes=["hflag"], dma_key="c3")
    add("dve", lambda e: e.tensor_copy(out=ident_b[:, :], in_=ident_f[:, :]), reads=["ident_f"], writes=["ident_b"])
    add("dve", lambda e: e.memset(ones_b[:, :], 1.0), writes=["ones_b"])
    add("dve", lambda e: e.memset(epsT[:, :], EPS), writes=["epsT"])
    for pc in range(2):
        add("sp", lambda e, pc=pc: e.dma_start(out=stage[:, :], in_=biasT_d[:, pc].rearrange("p h q -> p (h q)")),
            writes=["stage"], dma_key="c4")
        add("dve", lambda e, pc=pc: e.scalar_tensor_tensor(
            out=bias8[:, pc], in0=stage[:, :].rearrange("p (h q) -> p h q", h=16), scalar=8.0,
            in1=mask_f[:, pc:pc + 1, :].to_broadcast([128, 16, 128]), op0=ALU.mult, op1=ALU.add),
            reads=["stage", "sq"], writes=["bias8"])

    slab_steps = []
    state = {"issued": 0, "cur": -1, "mixer_done": -1}
    step_layer = []
    buf_of = []

    def plan_layer(li):
        for seg in range(2):
            for cg in COL_GROUPS:
                slab_steps.append(("kn", w_in[li][:, cg * 512:(cg + 1) * 512], "m"))
            for ms in range(4):
                slab_steps.append(("kn", w_out[li][:, ms * 512:(ms + 1) * 512], "m"))
        for g in range(NG):
            slab_steps.append(("kn", w_gate[li][:, g * 512:(g + 1) * 512], "f"))
            slab_steps.append(("kn", w_up[li][:, g * 512:(g + 1) * 512], "f"))
            slab_steps.append(("jn", w_down[li][g * 512:(g + 1) * 512, :], "f"))

    for li in range(nl):
        n0 = len(slab_steps)
        plan_layer(li)
        step_layer.extend([li] * (len(slab_steps) - n0))
    mcount = 0
    fcount = 0
    for kind, src, ring in slab_steps:
        if ring == "m":
            buf_of.append(mcount % 2)
            mcount += 1
        else:
            buf_of.append([2, 3, 0, 1][fcount % 4])
            fcount += 1
    step_emitted = [False] * len(slab_steps)

    def pump(look=3):
        while state["issued"] < len(slab_steps) and state["issued"] <= state["cur"] + look:
            s = state["issued"]
            b = buf_of[s]
            prev = [t for t in range(s) if buf_of[t] == b]
            if prev and not step_emitted[prev[-1]]:
                break
            if slab_steps[s][2] == "f" and b in (2, 3) and step_layer[s] > state["mixer_done"]:
                break
            kind, src, ring = slab_steps[s]
            subs = [("slab", b, q_) for q_ in range(4)]
            if b == 2:
                subs = subs + [("oab", s_, q_) for s_ in range(4) for q_ in range(4)]
            if kind == "kn":
                add("pool", lambda e, b=b, src=src: e.dma_start(out=slab_kn(b), in_=src.rearrange("(k p) n -> p k n", p=128)),
                    writes=[("slab", b)], dma_key=("slab", b), after=subs)
            else:
                for q_ in range(4):
                    add("pool", lambda e, b=b, src=src, q_=q_: e.dma_start(
                        out=slab_jn(b)[:, :, q_ * 512:(q_ + 1) * 512],
                        in_=src[:, q_ * 512:(q_ + 1) * 512].rearrange("(j p) n -> p j n", p=128)),
                        writes=[("slab", b, q_)], dma_key=("slab", b), after=[("slab", b)])
            state["issued"] += 1

    def begin_step():
        state["cur"] += 1
        s = state["cur"]
        assert state["issued"] > s, "slab not issued"
        return buf_of[s]

    def end_step():
        step_emitted[state["cur"]] = True

    tin = {"n": 0}

    def transpose_in(src_rows, dst, dkey, tokn=128):
        si = tin["n"] % 2
        tin["n"] += 1
        xst = slabs[:, 3, si * 4096:(si + 1) * 4096].bitcast(F32)
        add("sp", lambda e: e.dma_start(out=xst, in_=src_rows), writes=[("xs", si)], dma_key=("xin", si))
        for q4 in range(4):
            bank = 3 + q4
            for j in range(4):
                c = q4 * 4 + j
                add("pe", lambda e, bank=bank, j=j, c=c: e.transpose(ps[bank][:, j * 128:(j + 1) * 128], xst[:, c * 128:(c + 1) * 128], ident_f[:, :]),
                    reads=[("xs", si), "ident_f"], writes=[("ps", bank)])
            eng = "act" if q4 % 2 == 0 else "dve"
            if eng == "act":
                add("act", lambda e, bank=bank, q4=q4: e.copy(out=dst[:, q4 * 4:(q4 + 1) * 4, :], in_=ps[bank][:, :].rearrange("p (j t) -> p j t", j=4)),
                    reads=[("ps", bank)], writes=[dkey])
            else:
                add("dve", lambda e, bank=bank, q4=q4: e.tensor_copy(out=dst[:, q4 * 4:(q4 + 1) * 4, :], in_=ps[bank][:, :].rearrange("p (j t) -> p j t", j=4)),
                    reads=[("ps", bank)], writes=[dkey])

    ncount = {"n": 0}

    def norm_tile(src, skeys, gain, dst, dkeys, n, after=()):
        r = 0
        bank = 0
        for c in range(DC):
            xs = c % 2
            add("act", lambda e, c=c, xs=xs: e.activation(out=xsq[:, xs, 0:n], in_=src[:, c, :], func=AF.Square),
                reads=list(skeys), writes=[("xsq", xs)])
            add("pe", lambda e, c=c, xs=xs: e.matmul(ps[bank][:, 0:n], ones_b[:, :], xsq[:, xs, 0:n], start=(c == 0), stop=(c == DC - 1)),
                reads=[("xsq", xs), "ones_b"], writes=[("ps", bank)])
        add("act", lambda e: e.activation(out=scrA[:, r, 0:n], in_=ps[bank][:, 0:n], func=AF.Sqrt, bias=epsT[:, 0:1], scale=1.0 / D),
            reads=[("ps", bank), "epsT"], writes=[("scrA", r)])
        add("dve", lambda e: e.reciprocal(out=scrA[:, r, 0:n], in_=scrA[:, r, 0:n]),
            reads=[("scrA", r)], writes=[("scrA", r)])
        for c in range(DC):
            add("dve", lambda e, c=c: e.scalar_tensor_tensor(out=dst[:, c, :], in0=src[:, c, :], scalar=gain[:, c:c + 1], in1=scrA[:, r, 0:n],
                                                             op0=ALU.mult, op1=ALU.mult),
                reads=list(skeys) + [("scrA", r), "gains"], writes=list(dkeys), after=after)

    def headnorm(src, nh, gbc, outs, okeys, skey):
        w = nh * 64
        add("dve", lambda e: e.tensor_tensor(out=sq[:, 0:w], in0=src, in1=src, op=ALU.mult), reads=[skey], writes=["sq"])
        add("dve", lambda e: e.tensor_reduce(out=stat[:, 0:nh], in_=sq[:, 0:w].rearrange("p (h d) -> p h d", h=nh), axis=AX.X, op=ALU.add),
            reads=["sq"], writes=["stat"])
        add("act", lambda e: e.activation(out=stat[:, 0:nh], in_=stat[:, 0:nh], func=AF.Sqrt, bias=epsT[:, 0:1], scale=1.0 / 64),
            reads=["stat", "epsT"], writes=["stat"])
        add("dve", lambda e: e.reciprocal(out=stat[:, 0:nh], in_=stat[:, 0:nh]),
            reads=["stat"], writes=["stat"])
        s3 = src.rearrange("p (h d) -> p h d", h=nh)
        add("dve", lambda e: e.tensor_tensor(out=s3, in0=s3, in1=stat[:, 0:nh].unsqueeze(2).to_broadcast([128, nh, 64]), op=ALU.mult),
            reads=[skey, "stat"], writes=[skey])
        for o in outs:
            add("dve", lambda e, o=o: e.tensor_tensor(out=o, in0=s3, in1=gbc, op=ALU.mult),
                reads=[skey, "lconst"], writes=list(okeys))

    def layer_consts(li):
        add("dve", lambda e: e.memset(vaug[:, :, :, 64:66], 1.0), writes=[("slab", 3)], after=[("xs", 0), ("xs", 1)])
        add("dve", lambda e: e.tensor_copy(out=vaug[:, 0, :, 64:65], in_=hflag[:, :].unsqueeze(2)),
            reads=["hflag"], writes=[("slab", 3)])
        add("sp", lambda e: e.dma_start(out=g1[:, :], in_=g1_d[li]), writes=["gains"], dma_key="lc0")
        add("sp", lambda e: e.dma_start(out=g2[:, :], in_=g2_d[li]), writes=["gains"], dma_key="lc1")
        add("sp", lambda e: e.dma_start(out=gmix[:, :], in_=gmix_d[li]), writes=["gains"], dma_key="lc2")
        add("sp", lambda e: e.dma_start(out=sgub[:, :], in_=sgu_bT_d[li]), writes=["lconst"], dma_key="lc3")
        add("sp", lambda e: e.dma_start(out=qg[:, :], in_=qg_d[li:li + 1, :].partition_broadcast(128)), writes=["lconst"], dma_key="lc4")
        add("sp", lambda e: e.dma_start(out=kg[:, :], in_=kg_d[li:li + 1, :].partition_broadcast(128)), writes=["lconst"], dma_key="lc5")
        add("sp", lambda e: e.dma_start(out=esink[:, :], in_=sinks_d[li:li + 1, :].partition_broadcast(128)), writes=["esink"], dma_key="lc6")
        add("act", lambda e: e.activation(out=esink[:, :], in_=esink[:, :], func=AF.Exp), reads=["esink"], writes=["esink"])
        add("sp", lambda e: e.dma_start(out=stage[:, :], in_=sgu_wT_d[li].rearrange("p h t -> p (h t)")), writes=["stage"], dma_key="lc7")
        add("dve", lambda e: e.tensor_tensor(out=wT[:, :, :], in0=stage[:, :].rearrange("p (h t) -> p h t", h=16),
                                             in1=tril_f[:, :].unsqueeze(1).to_broadcast([128, 16, 128]), op=ALU.mult),
            reads=["stage", "tril_f"], writes=["wT"])

    def mixer_segment(li, seg, first_layer):
        b0 = seg * 4
        tok0 = b0 * 128
        if seg == 0:
            if first_layer:
                norm_tile(xTh, [("slab", 2)], g1, hT[:, :, 0:128], [("hT", 0)], 128, after=MIXK)
        norm_tile(xT[:, :, tok0:tok0 + 512], [("xT", seg)], g1, hT[:, :, 128:640], [("hT", 1)], 512, after=MIXK)

        _dbg("norm1_%d" % seg)
        items = []
        for cg in COL_GROUPS:
            slots = list(range(0 if (seg == 0) else 1, 5)) if cg == 6 else list(range(1, 5))
            for sl in slots:
                items.append((cg, sl, sl == slots[0], sl == slots[-1]))

        if DEBUG_ITEMS is not None and seg == 1:
            items = items[:DEBUG_ITEMS]
        cur = {"buf": None}
        zp_i = {"n": 0}

        def stage_A(it):
            cg, sl, first, lastb = it
            if first:
                cur["buf"] = begin_step()
                pump()
            b = cur["buf"]
            attn = cg in (4, 5)
            if attn or zp_i.get("prev_attn", True):
                bank = 0
            else:
                bank = 1 - zp_i["prev_bank"]
            zp_i["prev_attn"] = attn
            zp_i["prev_bank"] = bank
            W = slab_kn(b)
            hk = ("hT", 0 if sl == 0 else 1)
            for k in range(DC):
                add("pe", lambda e, k=k, bank=bank, W=W, sl=sl: e.matmul(ps[bank][:, :], hT[:, k, sl * 128:(sl + 1) * 128], W[:, k, :], start=(k == 0), stop=(k == DC - 1)),
                    reads=[hk, ("slab", b)], writes=[("ps", bank)])
            if lastb:
                end_step()
            return bank

        def outnorm_stats(sb_):
            okeys = [("oab", sb_, q_) for q_ in range(4)]
            for half in range(2):
                for pc_ in range(2):
                    lo_ = half * 1024 + pc_ * 512
                    add("dve", lambda e, lo_=lo_: e.tensor_tensor(out=sq[:, :], in0=out_ab[:, sb_, lo_:lo_ + 512], in1=out_ab[:, sb_, lo_:lo_ + 512], op=ALU.mult),
                        reads=okeys, writes=["sq"])
                    add("dve", lambda e, half=half, pc_=pc_: e.tensor_reduce(out=stat[:, 36 + 2 * half + pc_:37 + 2 * half + pc_], in_=sq[:, :], axis=AX.X, op=ALU.add),
                        reads=["sq"], writes=[("stat4", half, pc_)])
            add("dve", lambda e: e.tensor_reduce(out=stat[:, 32:34], in_=stat[:, 36:40].rearrange("p (a b) -> p a b", a=2), axis=AX.X, op=ALU.add),
                reads=[("stat4", 0, 0), ("stat4", 0, 1), ("stat4", 1, 0), ("stat4", 1, 1)], writes=[("stat3", 0), ("stat3", 1)])
            add("act", lambda e: e.activation(out=stat[:, 32:34], in_=stat[:, 32:34], func=AF.Sqrt, bias=epsT[:, 0:1], scale=1.0 / 1024),
                reads=[("stat3", 0), ("stat3", 1), "epsT"], writes=[("stat3", 0), ("stat3", 1)])
            add("dve", lambda e: e.reciprocal(out=stat[:, 32:34], in_=stat[:, 32:34]),
                reads=[("stat3", 0), ("stat3", 1)], writes=[("stat3", 0), ("stat3", 1)])
            for half in range(2):
                add("dve", lambda e, half=half: e.tensor_tensor(out=out_ab[:, sb_, half * 1024:(half + 1) * 1024], in0=out_ab[:, sb_, half * 1024:(half + 1) * 1024],
                                                                 in1=stat[:, 32 + half:33 + half].to_broadcast([128, 1024]), op=ALU.mult),
                    reads=okeys + [("stat3", half)], writes=okeys)

        def stage_B(it, bank, idx):
            cg, sl, first, lastb = it
            blk = b0 + sl - 1
            kidx = blk + 1
            r = idx % 2
            zk = ("zf", r)
            if cg == 6:
                add("act", lambda e: e.copy(out=zf[:, r, 0:256], in_=ps[bank][:, 0:256]), reads=[("ps", bank)], writes=[zk])
                add("act", lambda e: e.copy(out=vaug[:, kidx, :, 0:64], in_=ps[bank][:, 256:512].rearrange("p (k d) -> p k d", k=4)),
                    reads=[("ps", bank)], writes=[("slab", 3)])
                yield
                headnorm(zf[:, r, 0:256], 4, kg[:, :].unsqueeze(1).to_broadcast([128, 4, 64]),
                         [zn[:, r, 0:256].rearrange("p (k d) -> p k d", k=4)], [("zn", r)], zk)
                for kv in range(4):
                    add("pe", lambda e, kv=kv: e.transpose(psb(2)[0:64, r * 512 + kv * 128:r * 512 + (kv + 1) * 128], zn[:, r, kv * 64:(kv + 1) * 64], ident_b[:, :]),
                        reads=[("zn", r), "ident_b"], writes=[("ps", 2, 0), ("ps", 2, 1)])
                add("act", lambda e: e.copy(out=kT[0:64, kidx], in_=psb(2)[0:64, r * 512:(r + 1) * 512].rearrange("p (k t) -> p k t", k=4)),
                    reads=[("ps", 2, 0), ("ps", 2, 1)], writes=[("slab", 3)])
            elif cg in (4, 5):
                qh = cg - 4
                add("act", lambda e: e.copy(out=zf[:, r, :], in_=ps[bank][:, :]), reads=[("ps", bank)], writes=[zk])
                yield
                headnorm(zf[:, r, :], 8, qg[:, :].unsqueeze(1).to_broadcast([128, 8, 64]),
                         [zn[:, r, :].rearrange("p (h d) -> p h d", h=8)], [("zn", r)], zk)
                for hl in range(8):
                    add("pe", lambda e, hl=hl: e.transpose(psb(2)[0:64, hl * 128:(hl + 1) * 128], zn[:, r, hl * 64:(hl + 1) * 64], ident_b[:, :]),
                        reads=[("zn", r), "ident_b"], writes=[("ps", 2, 0), ("ps", 2, 1)])
                add("dve", lambda e: e.tensor_copy(out=qT[0:64, :, :], in_=psb(2)[0:64, :].rearrange("p (j t) -> p j t", j=8)),
                    reads=[("ps", 2, 0), ("ps", 2, 1)], writes=["qT"])
                for kvl in range(2):
                    kv = 2 * qh + kvl
                    for pc in range(2):
                        bk = 3 + 2 * kvl + pc
                        ki = kidx - 1 + pc
                        Sv = ps[bk][:, :].rearrange("p (g t) -> p g t", g=4)
                        add("pe", lambda e, bk=bk, pc=pc, kv=kv: e.matmul(ps[bk][:, :], ident_b[:, :], bias8[:, pc, 4 * kv:4 * kv + 4, :].rearrange("p g q -> p (g q)"), start=True, stop=False),
                            reads=["ident_b", "bias8"], writes=[("ps", bk)])
                        add("pe", lambda e, bk=bk, ki=ki, kv=kv, kvl=kvl: e.matmul(
                            ps[bk][:, :], kT[0:64, ki, kv, :], qT[0:64, 4 * kvl:4 * kvl + 4, :].rearrange("p j t -> p (j t)"),
                            start=False, stop=True),
                            reads=[("slab", 3), "qT"], writes=[("ps", bk)])
                        add("act", lambda e, bk=bk, kvl=kvl, pc=pc: e.activation(out=PT[:, kvl, pc, :], in_=ps[bk][:, :], func=AF.Exp, scale=0.125),
                            reads=[("ps", bk)], writes=[("PT", kvl, pc)])
                for kvl in range(2):
                    kv = 2 * qh + kvl
                    ob = 1 if kvl == 0 else 7
                    Ov = ps[ob][:, 0:264].rearrange("p (g d) -> p g d", g=4)
                    for g in range(4):
                        for pc in range(2):
                            ki = kidx - 1 + pc
                            add("pe", lambda e, Ov=Ov, g=g, pc=pc, ki=ki, kv=kv, kvl=kvl: e.matmul(
                                Ov[:, g, 0:65], PT[:, kvl, pc, g * 128:(g + 1) * 128], vaug[:, ki, kv, 0:65], start=(pc == 0), stop=(pc == 1)),
                                reads=[("PT", kvl, pc), ("slab", 3)], writes=[("ps", ob)])
                    sc = 16 + 4 * kvl
                    add("dve", lambda e, Ov=Ov, sc=sc, kv=kv: e.tensor_tensor(out=stat[:, sc:sc + 4], in0=Ov[:, :, 64:65].rearrange("p g o -> p (g o)"), in1=esink[:, 4 * kv:4 * kv + 4], op=ALU.add),
                        reads=[("ps", ob), "esink"], writes=[("stat2", kvl)])
                    add("dve", lambda e, sc=sc: e.reciprocal(out=stat[:, sc:sc + 4], in_=stat[:, sc:sc + 4]), reads=[("stat2", kvl)], writes=[("stat2", kvl)])
                    add("dve", lambda e, Ov=Ov, sc=sc, kv=kv: e.tensor_tensor(
                        out=out_ab[:, sl - 1, 1024 + kv * 256:1024 + (kv + 1) * 256].rearrange("p (g d) -> p g d", g=4),
                        in0=Ov[:, :, 0:64], in1=stat[:, sc:sc + 4].unsqueeze(2).to_broadcast([128, 4, 64]), op=ALU.mult),
                        reads=[("ps", ob), ("stat2", kvl)], writes=[("oab", sl - 1, 2 + kv // 2)], after=[("slab", 2)])
            elif cg in (2, 3):
                vh = cg - 2
                if first:
                    add("sp", lambda e: e.dma_start(out=sgbc[:, :], in_=sgu_g_d[li:li + 1, vh * 512:(vh + 1) * 512].partition_broadcast(128)),
                        writes=["sgbc"], dma_key="sgbc")
                add("act", lambda e: e.activation(out=zf[:, r, :], in_=ps[bank][:, :], func=AF.Gelu), reads=[("ps", bank)], writes=[zk])
                yield
                w = 512
                add("dve", lambda e: e.tensor_tensor(out=sq[:, 0:w], in0=zf[:, r, :], in1=zf[:, r, :], op=ALU.mult), reads=[zk], writes=["sq"])
                add("dve", lambda e: e.tensor_reduce(out=stat[:, 0:8], in_=sq[:, 0:w].rearrange("p (h d) -> p h d", h=8), axis=AX.X, op=ALU.add),
                    reads=["sq"], writes=["stat"])
                add("act", lambda e: e.activation(out=stat[:, 0:8], in_=stat[:, 0:8], func=AF.Sqrt, bias=epsT[:, 0:1], scale=1.0 / 64),
                    reads=["stat", "epsT"], writes=["stat"])
                add("dve", lambda e: e.reciprocal(out=stat[:, 0:8], in_=stat[:, 0:8]),
                    reads=["stat"], writes=["stat"])
                z3 = zf[:, r, :].rearrange("p (h d) -> p h d", h=8)
                add("dve", lambda e: e.tensor_tensor(out=z3, in0=z3, in1=stat[:, 0:8].unsqueeze(2).to_broadcast([128, 8, 64]), op=ALU.mult),
                    reads=[zk, "stat"], writes=[zk])
                add("dve", lambda e: e.tensor_tensor(out=zn[:, r, :], in0=zf[:, r, :], in1=sgbc[:, :], op=ALU.mult),
                    reads=[zk, "sgbc"], writes=[("zn", r)])
                gb = 3 + (idx % 2)
                for h in range(8):
                    add("pe", lambda e, h=h, gb=gb: e.matmul(ps[gb][:, h * 64:(h + 1) * 64], wT[:, vh * 8 + h, :], zn[:, r, h * 64:(h + 1) * 64], start=True, stop=True),
                        reads=["wT", ("zn", r)], writes=[("ps", gb)])
                add("dve", lambda e, gb=gb: e.tensor_tensor(out=gateS[:, sl - 1, :].rearrange("p (h d) -> p h d", h=8),
                                                             in0=ps[gb][:, :].rearrange("p (h d) -> p h d", h=8),
                                                             in1=sgub[:, vh * 8:(vh + 1) * 8].unsqueeze(2).to_broadcast([128, 8, 64]), op=ALU.add),
                    reads=[("ps", gb), "lconst"], writes=[("gateS", sl - 1)])
            else:
                uh = cg
                add("act", lambda e: e.activation(out=zf[:, r, :], in_=ps[bank][:, :], func=AF.Gelu), reads=[("ps", bank)], writes=[zk])
                yield
                add("dve", lambda e: e.tensor_tensor(out=out_ab[:, sl - 1, uh * 512:(uh + 1) * 512], in0=zf[:, r, :], in1=gateS[:, sl - 1, :], op=ALU.mult),
                    reads=[zk, ("gateS", sl - 1)], writes=[("oab", sl - 1, uh)], after=[("slab", 2)])
                if cg == COL_GROUPS[-1]:
                    outnorm_stats(sl - 1)

        pump(look=1)
        banks = {}
        banks[0] = stage_A(items[0])
        for i in range(len(items)):
            gen = stage_B(items[i], banks[i], i)
            next(gen)
            if i + 1 < len(items):
                banks[i + 1] = stage_A(items[i + 1])
            for _ in gen:
                pass

        _dbg("inproj%d" % seg)
        for sl in range(4):
            if DEBUG_ON < 3:
                continue
            for hb in range(2):
                bank = 3 + 2 * (sl % 2) + hb
                for j in range(8):
                    c = hb * 8 + j
                    add("pe", lambda e, bank=bank, j=j, c=c, sl=sl: e.transpose(psb(bank)[:, j * 128:(j + 1) * 128], out_ab[:, sl, c * 128:(c + 1) * 128], ident_b[:, :]),
                        reads=[("oab", sl, q_) for q_ in range(4)] + ["ident_b"], writes=[("ps", bank)])
                add("dve", lambda e, bank=bank, hb=hb, sl=sl: e.tensor_tensor(
                    out=mixT[:, hb * 8:(hb + 1) * 8, sl * 128:(sl + 1) * 128], in0=psb(bank)[:, :].rearrange("p (j t) -> p j t", j=8),
                    in1=gmix[:, hb * 8:(hb + 1) * 8].unsqueeze(2).to_broadcast([128, 8, 128]), op=ALU.mult),
                    reads=[("ps", bank), "gains"], writes=[("mixT", sl, hb * 8 + j_) for j_ in range(8)], after=[("hT", 0), ("hT", 1)])
        if seg == 1:
            state["mixer_done"] = li

        _dbg("outnorm%d" % seg)
        wb = [0, 1, 7]
        wi = 0
        for ms in range(4):
            b = begin_step()
            pump()
            W = slab_kn(b)
            for mi in range(4):
                m = ms * 4 + mi
                bank = wb[wi % 3]
                wi += 1
                for k in range(DC):
                    add("pe", lambda e, k=k, bank=bank, W=W, mi=mi: e.matmul(ps[bank][:, :], W[:, k, mi * 128:(mi + 1) * 128], mixT[:, k, :], start=(k == 0), stop=(k == DC - 1)),
                        reads=[("mixT", s_, k) for s_ in range(4)] + [("slab", b)], writes=[("ps", bank)])
                add("dve", lambda e, bank=bank, m=m: e.tensor_tensor(out=xT[:, m, tok0:tok0 + 512], in0=ps[bank][:, :], in1=xT[:, m, tok0:tok0 + 512], op=ALU.add),
                    reads=[("ps", bank), ("xT", seg)], writes=[("xT", seg)])
            end_step()

    def ffn(li):
        for t in range(2):
            norm_tile(xT[:, :, t * 512:(t + 1) * 512], [("xT", t)], g2, h2T[:, :, t * 512:(t + 1) * 512], [("h2T", t)], 512, after=MIXK + [("hT", 0), ("hT", 1)])
        gi = 0
        for g in range(NG):
            bg = begin_step()
            pump()
            bu = begin_step()
            pump()
            Wg = slab_kn(bg)
            Wu = slab_kn(bu)
            for t in range(2):
                for j in range(4):
                    gb = 1 + (gi % 2)
                    ub = 3 + (gi % 2)
                    r = gi % 2
                    gi += 1
                    for k in range(DC):
                        add("pe", lambda e, k=k, gb=gb, j=j, t=t, Wg=Wg: e.matmul(ps[gb][:, :], Wg[:, k, j * 128:(j + 1) * 128], h2T[:, k, t * 512:(t + 1) * 512], start=(k == 0), stop=(k == DC - 1)),
                            reads=[("h2T", t), ("slab", bg)], writes=pk(gb))
                    for k in range(DC):
                        add("pe", lambda e, k=k, ub=ub, j=j, t=t, Wu=Wu: e.matmul(ps[ub][:, :], Wu[:, k, j * 128:(j + 1) * 128], h2T[:, k, t * 512:(t + 1) * 512], start=(k == 0), stop=(k == DC - 1)),
                            reads=[("h2T", t), ("slab", bu)], writes=[("ps", ub)])
                    add("act", lambda e, gb=gb, r=r: e.activation(out=zf[:, r, :], in_=ps[gb][:, :], func=AF.Silu), reads=pk(gb), writes=[("zf", r)])
                    add("dve", lambda e, ub=ub, r=r, j=j, t=t: e.tensor_tensor(out=aT[:, j, t * 512:(t + 1) * 512], in0=ps[ub][:, :], in1=zf[:, r, :], op=ALU.mult),
                        reads=[("ps", ub), ("zf", r)], writes=["stage"])
            step_emitted[state["cur"] - 1] = True
            end_step()
            bd = begin_step()
            pump()
            Wd = slab_jn(bd)
            di = 0
            for t in range(2):
                for m in range(DC):
                    db = 5 + (di % 3)
                    di += 1
                    for j in range(4):
                        add("pe", lambda e, db=db, j=j, m=m, t=t, Wd=Wd: e.matmul(ps[db][:, :], Wd[:, j, m * 128:(m + 1) * 128], aT[:, j, t * 512:(t + 1) * 512], start=(j == 0), stop=(j == 3)),
                            reads=["stage", ("slab", bd)] + [("slab", bd, q_) for q_ in range(4)], writes=[("ps", db)])
                    add("dve", lambda e, db=db, m=m, t=t: e.tensor_tensor(out=xT[:, m, t * 512:(t + 1) * 512], in0=ps[db][:, :], in1=xT[:, m, t * 512:(t + 1) * 512], op=ALU.add),
                        reads=[("ps", db), ("xT", t)], writes=[("xT", t)])
            end_step()

    try:
        transpose_in(x_halo, xTh, ("slab", 2))
        for b in range(NB):
            transpose_in(x_own[b * 128:(b + 1) * 128, :], xT[:, :, b * 128:(b + 1) * 128], ("xT", b // 4))
        _dbg("load")
        for li in range(nl):
            layer_consts(li)
            for seg in range(2):
                mixer_segment(li, seg, li == 0)
                _dbg("seg%d" % seg)
            P.barrier()
            _dbg("mixer")
            ffn(li)
            P.barrier()
    except _Stop:
        pass
    oi = 0
    for b in range(NB):
        ost = slabs[:, b // 2, (b % 2) * 4096:(b % 2 + 1) * 4096].bitcast(F32)
        for q4 in range(4):
            bank = 3 + (oi % 4)
            oi += 1
            for j in range(4):
                c = q4 * 4 + j
                add("pe", lambda e, bank=bank, j=j, c=c, b=b: e.transpose(ps[bank][:, j * 128:(j + 1) * 128], xT[:, c, b * 128:(b + 1) * 128], ident_f[:, :]),
                    reads=[("xT", b // 4), "ident_f"], writes=[("ps", bank)])
            oa = [("slab", b // 2)] + [("slab", b // 2, q_) for q_ in range(4)]
            if q4 % 2 == 0:
                add("act", lambda e, bank=bank, q4=q4, ost=ost: e.copy(out=ost[:, q4 * 512:(q4 + 1) * 512], in_=ps[bank][:, :]),
                    reads=[("ps", bank)], writes=[("ost", b, q4)], after=oa)
            else:
                add("dve", lambda e, bank=bank, q4=q4, ost=ost: e.tensor_copy(out=ost[:, q4 * 512:(q4 + 1) * 512], in_=ps[bank][:, :]),
                    reads=[("ps", bank)], writes=[("ost", b, q4)], after=oa)
        add("sp", lambda e, b=b, ost=ost: e.dma_start(out=y[b * 128:(b + 1) * 128, :], in_=ost),
            reads=[("ost", b, q_) for q_ in range(4)], writes=[("y", b)], dma_key="yout")
    add("sp", None, reads=[("y", b) for b in range(NB)], writes=["y_done"])
    P.emit(nc, st)
    st.close()
    return nc


def _bucket_table():
    q = np.arange(128)[:, None]
    k = np.arange(256)[None, :]
    dist = q + 128 - k
    n = np.maximum(dist, 0)
    max_exact = 16
    nf = np.maximum(n, 1).astype(np.float32)
    large = max_exact + (np.log(nf / max_exact) / np.log(128 / max_exact) * (32 - max_exact)).astype(np.int32)
    large = np.minimum(large, 31)
    bucket = np.where(n < max_exact, n, large)
    valid = (dist >= 0) & (dist < 128)
    return bucket, valid


_NC_CACHE = {}


def _get_nc(nl):
    if nl not in _NC_CACHE:
        _NC_CACHE[nl] = build(nl, True)
    return _NC_CACHE[nl]


def kernel(x, rel_bias, norm1_g, w_in, sgu_norm_g, sgu_w, sgu_b, q_norm_g, k_norm_g, sinks,
           out_norm_a, out_norm_b, w_out, norm2_g, w_gate, w_up, w_down):
    f32 = np.float32
    x = np.asarray(x, f32)
    L = w_in.shape[0]
    bucket, valid = _bucket_table()
    bias_g = np.asarray(rel_bias, f32)[bucket]
    biasT = np.ascontiguousarray(bias_g.reshape(128, 2, 128, 16).transpose(2, 1, 3, 0))
    maskT = np.ascontiguousarray(np.where(valid, 0.0, NEG).astype(f32).reshape(128, 2, 128).transpose(2, 1, 0))
    trilT = np.ascontiguousarray(np.triu(np.ones((128, 128), f32)))
    ident = np.eye(128, dtype=f32)

    def chunked(v):
        return np.ascontiguousarray(np.asarray(v, f32).reshape(L, DC, 128).transpose(0, 2, 1))

    g1 = chunked(norm1_g)
    g2 = chunked(norm2_g)
    gmix = chunked(np.concatenate([np.asarray(out_norm_a, f32), np.asarray(out_norm_b, f32)], axis=1))
    sgu_wT = np.ascontiguousarray(np.asarray(sgu_w, f32).transpose(0, 3, 1, 2))
    sgu_bT = np.ascontiguousarray(np.asarray(sgu_b, f32).transpose(0, 2, 1))
    sgu_g = np.ascontiguousarray(np.asarray(sgu_norm_g, f32).reshape(L, 1024))

    xs = x.reshape(SEQ, D)
    nl = 1
    nc = _get_nc(nl)
    for l in range(L):
        in_maps = []
        for c in range(NCORE):
            own = xs[c * TOK:(c + 1) * TOK]
            halo = xs[c * TOK - 128:c * TOK] if c > 0 else np.zeros((128, D), f32)
            flag = np.full((128, 4), 1.0 if c > 0 else 0.0, f32)
            in_maps.append({
                "x_own": np.ascontiguousarray(own), "x_halo": np.ascontiguousarray(halo), "halo_flag": flag,
                "ident": ident, "trilT": trilT, "biasT": biasT, "maskT": maskT,
                "w_in": np.asarray(w_in[l:l + 1], f32), "w_out": np.asarray(w_out[l:l + 1], f32),
                "w_gate": np.asarray(w_gate[l:l + 1], f32), "w_up": np.asarray(w_up[l:l + 1], f32),
                "w_down": np.asarray(w_down[l:l + 1], f32),
                "sgu_wT": sgu_wT[l:l + 1], "g1": g1[l:l + 1], "g2": g2[l:l + 1], "gmix": gmix[l:l + 1],
                "sgu_g": sgu_g[l:l + 1], "sgu_bT": sgu_bT[l:l + 1],
                "qg": np.asarray(q_norm_g[l:l + 1], f32), "kg": np.asarray(k_norm_g[l:l + 1], f32),
                "sinks": np.asarray(sinks[l:l + 1], f32),
            })
        res = run_bass_kernel_spmd(nc, in_maps, core_ids=list(range(NCORE)))
        xs = np.concatenate([np.asarray(r["y"], f32) for r in res.results], axis=0)
    return xs.reshape(1, SEQ, D)
```
